# Optimizing a Trainium2 kernel written in Bass

```python
import math
import jax, jax.numpy as jnp
from jax import lax
import numpy as np

D_MODEL = 1024
BATCH = 8
SEQ = 4096
DEPTH = 2

N_MIXERS = 2
HEAD_DIM = 64
DIFF_HEADS = D_MODEL // (2 * HEAD_DIM)
DIFF_V_DIM = 2 * HEAD_DIM
DIFF_LAMBDA_STD = 0.1
MOBA_HEADS = D_MODEL // HEAD_DIM
MOBA_BLOCK = 256
MOBA_TOPK = 3
MOBA_Q_CHUNK = 32
Q_BLOCK = 128
REL_BUCKETS = 32
REL_MAX_DIST = 128
N_BIAS_COLS = MOBA_HEADS
D_FF = int(math.ceil(8 * D_MODEL / 3 / 256)) * 256
FFN_RESIDUAL = 0.5
RMS_EPS = 1e-6
SUBLN_EPS = 1e-5

kernel_name = "hybrid_diffattn_moba_macaron"


def rms_norm(x, g, eps=RMS_EPS):
    xf = x.astype(jnp.float32)
    y = xf * lax.rsqrt(jnp.mean(xf * xf, axis=-1, keepdims=True) + eps)
    return (y * g.astype(jnp.float32)).astype(x.dtype)


def swiglu(h, w_in, w_out):
    g, u = jnp.split(h @ w_in, 2, axis=-1)
    return (jax.nn.silu(g) * u) @ w_out


def rel_bucket(dist):
    n = jnp.maximum(dist, 0)
    max_exact = REL_BUCKETS // 2
    nf = jnp.maximum(n, 1).astype(jnp.float32)
    large = max_exact + (jnp.log(nf / max_exact) / math.log(REL_MAX_DIST / max_exact)
                         * (REL_BUCKETS - max_exact)).astype(jnp.int32)
    large = jnp.minimum(large, REL_BUCKETS - 1)
    return jnp.where(n < max_exact, n, large)


def diff_attention(h, w_qkv, lam_params, subln_g, w_o, rel_bias, layer_idx):
    B, S, _ = h.shape
    H, d = DIFF_HEADS, HEAD_DIM
    q, k, v = jnp.split(h @ w_qkv, 3, axis=-1)
    q = q.reshape(B, S, 2 * H, d).transpose(0, 2, 1, 3)
    k = k.reshape(B, S, 2 * H, d).transpose(0, 2, 1, 3)
    v = v.reshape(B, S, H, DIFF_V_DIM).transpose(0, 2, 1, 3)
    lp = lam_params.astype(jnp.float32)
    lam_init = 0.8 - 0.6 * math.exp(-0.3 * layer_idx)
    lam = jnp.exp(jnp.sum(lp[0] * lp[1])) - jnp.exp(jnp.sum(lp[2] * lp[3])) + lam_init
    n_qb = S // Q_BLOCK
    q_blocks = q.reshape(B, 2 * H, n_qb, Q_BLOCK, d).transpose(2, 0, 1, 3, 4)
    k_pos = jnp.arange(S)
    scale = d ** -0.5

    def block(args):
        qb_idx, qb = args
        q_pos = qb_idx * Q_BLOCK + jnp.arange(Q_BLOCK)
        dist = q_pos[:, None] - k_pos[None, :]
        bias = jnp.moveaxis(rel_bias[rel_bucket(dist)], -1, 0).astype(jnp.float32)
        s = jnp.einsum('bmqd,bmkd->bmqk', qb, k).astype(jnp.float32) * scale + bias
        s = jnp.where(dist >= 0, s, -jnp.inf)
        p = jax.nn.softmax(s, axis=-1).reshape(B, H, 2, Q_BLOCK, S)
        a = p[:, :, 0] - lam * p[:, :, 1]
        return jnp.einsum('bhqk,bhkv->bhqv', a.astype(v.dtype), v)

    o = lax.map(block, (jnp.arange(n_qb), q_blocks))
    o = o.transpose(1, 2, 0, 3, 4).reshape(B, H, S, DIFF_V_DIM)
    o = rms_norm(o, subln_g, SUBLN_EPS) * (1.0 - lam_init)
    o = o.transpose(0, 2, 1, 3).reshape(B, S, H * DIFF_V_DIM)
    return o @ w_o


def moba_attention(h, w_qkv, w_o, rel_bias):
    B, S, _ = h.shape
    H, d, L, C = MOBA_HEADS, HEAD_DIM, MOBA_BLOCK, MOBA_Q_CHUNK
    q, k, v = jnp.split(h @ w_qkv, 3, axis=-1)
    q = q.reshape(B, S, H, d).transpose(0, 2, 1, 3)
    k = k.reshape(B, S, H, d).transpose(0, 2, 1, 3)
    v = v.reshape(B, S, H, d).transpose(0, 2, 1, 3)
    n_blk = -(-S // L)
    S_pad = n_blk * L
    pad = ((0, 0), (0, 0), (0, S_pad - S), (0, 0))
    q, k, v = jnp.pad(q, pad), jnp.pad(k, pad), jnp.pad(v, pad)
    kb = k.reshape(B, H, n_blk, L, d)
    vb = v.reshape(B, H, n_blk, L, d)
    k_mean = jnp.mean(kb.astype(jnp.float32), axis=3)
    gate = jnp.einsum('bhsd,bhnd->bhsn', q.astype(jnp.float32), k_mean)
    q_blk = jnp.arange(S_pad) // L
    eligible = jnp.arange(n_blk)[None, :] < q_blk[:, None]
    gate = jnp.where(eligible, gate, -jnp.inf)
    n_sel = min(MOBA_TOPK, n_blk)
    _, sel = lax.top_k(gate, n_sel)
    valid = sel < q_blk[:, None]
    n_ch = S_pad // C

    def to_chunks(a):
        return a.reshape(B, H, n_ch, C, a.shape[-1]).transpose(2, 0, 1, 3, 4)

    b_idx = jnp.arange(B)[:, None, None, None]
    h_idx = jnp.arange(H)[None, :, None, None]
    bias_t = rel_bias.T.astype(jnp.float32)
    scale = d ** -0.5

    def chunk(args):
        c, qc, selc, validc = args
        q_pos = c * C + jnp.arange(C)
        own = (c * C) // L
        kg = kb[b_idx, h_idx, selc]
        vg = vb[b_idx, h_idx, selc]
        kpos_past = selc[..., None] * L + jnp.arange(L)
        bias_past = bias_t[h_idx[..., None], rel_bucket(q_pos[:, None, None] - kpos_past)]
        s_past = jnp.einsum('bhqd,bhqnld->bhqnl', qc, kg).astype(jnp.float32) * scale + bias_past
        s_past = jnp.where(validc[..., None], s_past, -jnp.inf)
        ko = lax.dynamic_index_in_dim(kb, own, axis=2, keepdims=False)
        vo = lax.dynamic_index_in_dim(vb, own, axis=2, keepdims=False)
        dist_own = q_pos[:, None] - (own * L + jnp.arange(L))[None, :]
        bias_own = jnp.moveaxis(bias_t.T[rel_bucket(dist_own)], -1, 0)
        s_own = jnp.einsum('bhqd,bhld->bhql', qc, ko).astype(jnp.float32) * scale + bias_own
        s_own = jnp.where(dist_own >= 0, s_own, -jnp.inf)
        s = jnp.concatenate([s_past.reshape(B, H, C, n_sel * L), s_own], axis=-1)
        p = jax.nn.softmax(s, axis=-1).astype(v.dtype)
        p_past = p[..., :n_sel * L].reshape(B, H, C, n_sel, L)
        p_own = p[..., n_sel * L:]
        return (jnp.einsum('bhqnl,bhqnld->bhqd', p_past, vg)
                + jnp.einsum('bhql,bhld->bhqd', p_own, vo))

    o = lax.map(chunk, (jnp.arange(n_ch), to_chunks(q), to_chunks(sel), to_chunks(valid)))
    o = o.transpose(1, 0, 3, 2, 4).reshape(B, S_pad, H * d)[:, :S]
    return o @ w_o


def setup_inputs(seed: int = 0) -> dict:
    key = jax.random.key(seed)
    ks = jax.random.split(key, 12)
    D, F = D_MODEL, D_FF
    n_a = len(range(0, DEPTH, N_MIXERS))
    n_b = len(range(1, DEPTH, N_MIXERS))
    nrm = jax.random.normal
    f32 = jnp.float32
    return {
        'x': nrm(ks[0], (BATCH, SEQ, D), f32),
        'rel_bias': 0.2 * nrm(ks[1], (REL_BUCKETS, N_BIAS_COLS), f32),
        'norm_g': 1.0 + 0.02 * nrm(ks[2], (DEPTH, 3, D), f32),
        'final_norm_g': 1.0 + 0.02 * nrm(ks[3], (D,), f32),
        'ffn_w_in': nrm(ks[4], (DEPTH, 2, D, 2 * F), f32) * D ** -0.5,
        'ffn_w_out': nrm(ks[5], (DEPTH, 2, F, D), f32) * F ** -0.5,
        'diff_w_qkv': nrm(ks[6], (n_a, D, 3 * DIFF_HEADS * DIFF_V_DIM), f32) * D ** -0.5,
        'diff_lambda': DIFF_LAMBDA_STD * nrm(ks[7], (n_a, 4, HEAD_DIM), f32),
        'diff_subln_g': 1.0 + 0.02 * nrm(ks[8], (n_a, DIFF_V_DIM), f32),
        'diff_w_o': nrm(ks[9], (n_a, DIFF_HEADS * DIFF_V_DIM, D), f32) * (DIFF_HEADS * DIFF_V_DIM) ** -0.5,
        'moba_w_qkv': nrm(ks[10], (n_b, D, 3 * MOBA_HEADS * HEAD_DIM), f32) * D ** -0.5,
        'moba_w_o': nrm(ks[11], (n_b, MOBA_HEADS * HEAD_DIM, D), f32) * (MOBA_HEADS * HEAD_DIM) ** -0.5,
    }


def reference(x, rel_bias, norm_g, final_norm_g, ffn_w_in, ffn_w_out, diff_w_qkv, diff_lambda,
              diff_subln_g, diff_w_o, moba_w_qkv, moba_w_o):
    h = x
    for i in range(DEPTH):
        g = norm_g[i]
        h = h + FFN_RESIDUAL * swiglu(rms_norm(h, g[0]), ffn_w_in[i, 0], ffn_w_out[i, 0])
        hn = rms_norm(h, g[1])
        j = i // N_MIXERS
        if i % N_MIXERS == 0:
            mix = diff_attention(hn, diff_w_qkv[j], diff_lambda[j], diff_subln_g[j], diff_w_o[j], rel_bias, i)
        else:
            mix = moba_attention(hn, moba_w_qkv[j], moba_w_o[j], rel_bias)
        h = h + mix
        h = h + FFN_RESIDUAL * swiglu(rms_norm(h, g[2]), ffn_w_in[i, 1], ffn_w_out[i, 1])
    return rms_norm(h, final_norm_g)
```

```python
import numpy as np
import concourse.bass as bass
import concourse.mybir as mybir
from concourse.bass_utils import run_bass_kernel_spmd

F32 = mybir.dt.float32
BF16 = mybir.dt.bfloat16
AF = mybir.ActivationFunctionType
ALU = mybir.AluOpType
AX = mybir.AxisListType

D = 1024
FF = 2816
NKC = 8
NFC = 22
SEQ = 4096
NCORES = 8
RMS_EPS = 1e-6
SUBLN_EPS = 1e-5
NEG_BIG = -30000.0


class Res:
    __slots__ = ("name", "w", "rs")

    def __init__(self, name=""):
        self.name = name
        self.w = None
        self.rs = []


class Op:
    __slots__ = ("eng", "fn", "deps", "ms", "need", "dma", "sem_idx", "semval")

    def __init__(self, eng, fn, dma):
        self.eng = eng
        self.fn = fn
        self.deps = ()
        self.ms = 0
        self.need = False
        self.dma = dma
        self.sem_idx = -1
        self.semval = 0


ENGS = ("pe", "act", "dve", "pool", "sp")


class Sched:
    def __init__(self, n_dma_sems=48):
        self.ops = {e: [] for e in ENGS}
        self.last_real = {}
        self.dmas_since_barrier = []
        self.dma_ops = []
        self.NS = n_dma_sems

    def add(self, eng, fn, reads=(), writes=(), dma=False):
        op = Op(eng, fn, dma)
        raw = set()
        other = set()
        for r in reads:
            if r.w is not None:
                raw.add(r.w)
        for w in writes:
            if w.w is not None:
                other.add(w.w)
            for rd in w.rs:
                other.add(rd)
        deps = set()
        for dpo in raw | other:
            if dpo is op:
                continue
            if dpo.dma or dma:
                deps.add(dpo)
            elif dpo.eng == eng:
                if eng != "pe" and dpo in raw:
                    deps.add(dpo)
            else:
                deps.add(dpo)
        if dma:
            k = len(self.dma_ops)
            op.sem_idx = k % self.NS
            op.semval = 16 * (k // self.NS + 1)
            if k >= self.NS:
                deps.add(self.dma_ops[k - self.NS])
            self.dma_ops.append(op)
            self.dmas_since_barrier.append(op)
        for dpo in deps:
            dpo.need = True
        op.deps = tuple(deps)
        for r in reads:
            r.rs.append(op)
        for w in writes:
            w.w = op
            w.rs = []
        self.ops[eng].append(op)
        if fn is not None and not dma:
            self.last_real[eng] = op
        return op

    def barrier(self):
        lasts = list(self.last_real.values()) + list(self.dmas_since_barrier)
        for e in ENGS:
            op = Op(e, None, False)
            deps = []
            for dpo in lasts:
                if (not dpo.dma) and dpo.eng == e:
                    continue
                dpo.need = True
                deps.append(dpo)
            op.deps = tuple(deps)
            self.ops[e].append(op)
        self.dmas_since_barrier = []

    def emit(self, nc):
        for e in ENGS:
            c = 0
            for op in self.ops[e]:
                if op.dma or op.fn is None:
                    continue
                if op.need:
                    c += 1
                op.ms = c
        import contextlib
        with contextlib.ExitStack() as st:
            esem = {e: st.enter_context(nc.semaphore("s_" + e)) for e in ("pe", "act", "dve", "pool")}
            dsem = [st.enter_context(nc.semaphore("d%d" % i)) for i in range(self.NS)]
            block = st.enter_context(nc.Block())

            def run(ename):
                def body(eng):
                    seen = {}
                    for op in self.ops[ename]:
                        for dpo in op.deps:
                            if dpo.dma:
                                key = ("d", dpo.sem_idx)
                                val = dpo.semval
                                sem = dsem[dpo.sem_idx]
                            else:
                                key = dpo.eng
                                val = dpo.ms
                                sem = esem[dpo.eng]
                            if seen.get(key, 0) >= val:
                                continue
                            eng.wait_ge(sem, val)
                            seen[key] = val
                        if op.fn is None:
                            continue
                        ins = op.fn(eng)
                        if op.dma:
                            ins.then_inc(dsem[op.sem_idx], 16)
                        elif op.need:
                            ins.then_inc(esem[ename], 1)
                return body

            block.tensor(run("pe"))
            block.scalar(run("act"))
            block.vector(run("dve"))
            block.gpsimd(run("pool"))
            block.sync(run("sp"))


class Builder:
    def __init__(self, S, stop_after=None, debug=False):
        self.S = S
        self.stop_after = stop_after
        self.debug = debug
        nc = bass.Bass("TRN2", target_bir_lowering=False)
        self.nc = nc
        self.s = Sched()
        lo, hi = nc.bump_sbuf(212000)
        self.sb_lo, self.sb_hi = lo, hi
        self.sb_off = lo
        self.cnt = 0
        self.psum = [nc.alloc_psum_tensor("bank%d" % i, [128, 512], F32) for i in range(8)]
        self.pres = [Res("bank%d" % i) for i in range(8)]

    def sb(self, shape, dtype, name="t"):
        esz = 4 if dtype == F32 else 2
        n = 1
        for d in shape[1:]:
            n *= d
        nbytes = (n * esz + 31) // 32 * 32
        off = self.sb_off
        assert off + nbytes <= self.sb_hi, "SBUF overflow %s %d" % (name, off + nbytes - self.sb_hi)
        self.sb_off += nbytes
        self.cnt += 1
        return self.nc.alloc_sbuf_tensor_at("%s_%d" % (name, self.cnt), list(shape), dtype, offset=off)

    def mark(self):
        return self.sb_off

    def release(self, m):
        self.sb_off = m

    def dram_in(self, name, shape, dtype=F32):
        return self.nc.dram_tensor(name, list(shape), dtype, kind="ExternalInput").ap()

    def dram_tmp(self, name, shape, dtype):
        kind = "ExternalOutput" if self.debug else "Internal"
        return self.nc.dram_tensor(name, list(shape), dtype, kind=kind).ap()

    def dma(self, q, out, in_, reads, writes, **kw):
        return self.s.add(q, lambda e, out=out, in_=in_, kw=kw: e.dma_start(out=out, in_=in_, **kw),
                          reads, writes, dma=True)

    def op(self, eng, fn, reads=(), writes=()):
        return self.s.add(eng, fn, reads, writes)

    def conv_schedule(self, key, src, dst):
        R = src.shape[0]
        res = Res("wb_" + key)
        self.r_wb[key] = res
        for r0 in range(0, R, 128):
            self.conv_queue.append((src[r0:r0 + 128, :], dst[r0:r0 + 128, :], res))

    def conv_pump(self, n=None):
        n = len(self.conv_queue) if n is None else n
        for _ in range(min(n, len(self.conv_queue))):
            s_, d_, res = self.conv_queue.pop(0)
            self.dma("pool", d_, s_, [], [res], max_dma_last_dim=8192)

    def declare_io(self):
        S = self.S
        self.x = self.dram_in("x", [S, D])
        self.rel_bias = self.dram_in("rel_bias", [32, 16])
        self.norm_g = self.dram_in("norm_g", [2, 3, D])
        self.final_g = self.dram_in("final_norm_g", [1, D])
        self.ffn_w_in = self.dram_in("ffn_w_in", [2, 2, D, 2 * FF])
        self.ffn_w_out = self.dram_in("ffn_w_out", [2, 2, FF, D])
        self.diff_w_qkv = self.dram_in("diff_w_qkv", [D, 3 * D])
        self.diff_lambda = self.dram_in("diff_lambda", [1, 256])
        self.diff_subln_g = self.dram_in("diff_subln_g", [1, 128])
        self.diff_w_o = self.dram_in("diff_w_o", [D, D])
        self.moba_w_qkv = self.dram_in("moba_w_qkv", [D, 3 * D])
        self.moba_w_o = self.dram_in("moba_w_o", [D, D])
        self.c_ident = self.dram_in("c_ident", [128, 128])
        self.c_anti = self.dram_in("c_anti", [128, 128])
        self.c_oh = self.dram_in("c_oh", [32, 256])
        self.c_ind = self.dram_in("c_ind", [16, S])
        self.out = self.nc.dram_tensor("out", [S, D], F32, kind="ExternalOutput").ap()
        self.h = self.dram_tmp("h_res", [S, D], F32)
        self.QT = self.dram_tmp("QT", [D, S], BF16)
        self.KT = self.dram_tmp("KT", [D, S], BF16)
        self.VA = self.dram_tmp("VA", [S, 1040], BF16)
        self.OT = self.dram_tmp("OT", [D, S], BF16)
        self.PT = self.dram_tmp("PT", [16, 16, S], BF16)
        self.EGd = self.dram_tmp("EGd", [16, 384], F32)
        self.DEd = self.dram_tmp("DEd", [2, 128, 16, 128], BF16)
        self.wb_in = [self.dram_tmp("wb_in%d" % i, [D, 2 * FF], BF16) for i in range(2)]
        self.wb_out = [self.dram_tmp("wb_out%d" % i, [FF, D], BF16) for i in range(2)]
        self.wb_o = [self.dram_tmp("wb_o%d" % i, [D, D], BF16) for i in range(2)]
        self.wb_qkv = self.dram_tmp("wb_qkv", [D, 3 * D], BF16)
        self.r_wb = {}
        self.conv_queue = []
        self.r_h = Res("h")
        self.r_qkv = Res("qkv")
        self.r_ot = Res("ot")
        self.r_pt = Res("pt")

    def setup_consts(self):
        nc = self.nc
        self.identb = self.sb([128, 128], BF16, "identb")
        self.r_const = Res("const")
        idf = self.sb([128, 128], F32, "identf")
        r_idf = Res("idf")
        self.dma("sp", idf[:], self.c_ident, [], [r_idf])
        self.op("dve", lambda e: e.tensor_copy(out=self.identb[:], in_=idf[:]), [r_idf], [self.r_const])
        self.neghalf = self.sb([128, 1], F32, "neghalf")
        self.op("pool", lambda e: e.memset(self.neghalf[:], -0.5), [], [self.r_const])
        self.onescol = self.sb([128, 1], F32, "ones")
        self.op("pool", lambda e: e.memset(self.onescol[:], 1.0), [], [self.r_const])
        self.epscol = self.sb([128, 1], F32, "epscol")
        self.op("pool", lambda e: e.memset(self.epscol[:], SUBLN_EPS), [], [self.r_const])

    def norm_bufs(self, n=2):
        nb = {"n": n, "i": 0}
        nb["h"] = [self.sb([128, D], F32, "nh") for _ in range(n)]
        nb["xn"] = [self.sb([128, D], BF16, "nxn") for _ in range(n)]
        nb["ss"] = [self.sb([128, 1], F32, "nss") for _ in range(n)]
        nb["ms"] = [self.sb([128, 1], F32, "nms") for _ in range(n)]
        nb["rstd"] = [self.sb([128, 1], F32, "nrs") for _ in range(n)]
        nb["rh"] = [Res("nh%d" % i) for i in range(n)]
        nb["rxn"] = [Res("nxn%d" % i) for i in range(n)]
        nb["rss"] = [Res("nss%d" % i) for i in range(n)]
        nb["rrs"] = [Res("nrs%d" % i) for i in range(n)]
        return nb

    def norm_pre(self, nb, src_rows, src_res, gb, r_gb, eps=RMS_EPS):
        i = nb["i"] % nb["n"]
        nb["i"] += 1
        hb, xn, ss, ms, rstd = nb["h"][i], nb["xn"][i], nb["ss"][i], nb["ms"][i], nb["rstd"][i]
        rh, rxn, rss, rrs = nb["rh"][i], nb["rxn"][i], nb["rss"][i], nb["rrs"][i]
        self.dma("sp", hb[:], src_rows, src_res, [rh])
        self.op("act", lambda e: e.activation(out=xn[:], in_=hb[:], func=AF.Square, accum_out=ss[:]),
                [rh], [rxn, rss])

        self.op("pool", lambda e: e.tensor_scalar(out=ms[:], in0=ss[:], scalar1=1.0 / D, scalar2=eps, op0=ALU.mult,
                                                  op1=ALU.add), [rss], [rrs])
        self.op("pool", lambda e: e.tensor_tensor(out=rstd[:], in0=ms[:], in1=self.neghalf[:], op=ALU.pow),
                [rrs, self.r_const], [rrs])
        self.op("dve", lambda e: e.scalar_tensor_tensor(out=xn[:], in0=hb[:], scalar=rstd[:], in1=gb[:],
                                                        op0=ALU.mult, op1=ALU.mult),
                [rh, rrs, r_gb], [rxn])
        return i

    def norm_T(self, nb, i, trbank, dst, r_dst, copy_eng="act"):
        xn, rxn = nb["xn"][i], nb["rxn"][i]
        ps = self.psum[trbank][:].bitcast(BF16).rearrange("p (k t) -> p k t", k=8)
        pr = self.pres[trbank]

        def f_tr(e):
            ins = None
            for kc in range(NKC):
                ins = e.transpose(out=ps[:, kc, :], in_=xn[:, kc * 128:(kc + 1) * 128], identity=self.identb[:])
            return ins
        self.op("pe", f_tr, [rxn, self.r_const], [pr])
        if copy_eng == "act":
            self.op("act", lambda e: e.copy(out=dst, in_=ps), [pr], [r_dst])
        else:
            self.op("dve", lambda e: e.tensor_copy(out=dst, in_=ps), [pr], [r_dst])

    def load_gain(self, gb, r_gb, src_row):
        self.dma("sp", gb[:], src_row.partition_broadcast(128), [], [r_gb])

    def ffn_load(self, l, which):
        w_in = self.ffn_w_in[l, which]
        w_out = self.ffn_w_out[l, which]
        Win = self.sb([128, NKC, 2 * FF], BF16, "Win")
        Wout = self.sb([128, NFC, D], BF16, "Wout")
        r_win = [Res("win%d" % k) for k in range(NKC)]
        r_wout = Res("wout")
        for kc in range(NKC):
            self.dma("pool", Win[:, kc, :], w_in[kc * 128:(kc + 1) * 128, :], [], [r_win[kc]],
                     max_dma_last_dim=8192)
        w_out_v = w_out.rearrange("(k p) d -> p k d", p=128)
        for k0 in range(0, NFC, 6):
            k1 = min(NFC, k0 + 6)
            self.dma("pool", Wout[:, k0:k1, :], w_out_v[:, k0:k1, :], [], [r_wout])
        return Win, Wout, r_win, r_wout

    def wo_load(self, li):
        w = self.diff_w_o if li == 0 else self.moba_w_o
        Wo = self.sb([128, NKC, D], BF16, "Wo")
        r_wo = Res("wo")
        wv = w.rearrange("(k p) d -> p k d", p=128)
        for k0 in range(0, NKC, 4):
            self.dma("pool", Wo[:, k0:k0 + 4, :], wv[:, k0:k0 + 4, :], [], [r_wo])
        return Wo, r_wo

    def ffn_phase(self, h_in, l, which, final=False, w=None, oproj=None):
        S = self.S
        T = 256
        NT = S // T
        m0 = self.mark()
        gidx = 0 if which == 0 else 2
        if w is None:
            Win, Wout, r_win, r_wout = self.ffn_load(l, which)
        else:
            Win, Wout, r_win, r_wout = w
        gb = self.sb([128, D], F32, "gb")
        r_gb = Res("gb")
        self.load_gain(gb, r_gb, self.norm_g[l, gidx:gidx + 1, :])
        if final:
            gfb = self.sb([128, D], F32, "gfb")
            r_gfb = Res("gfb")
            self.load_gain(gfb, r_gfb, self.final_g[0:1, :])
        hb4 = [self.sb([128, D], F32, "hb") for _ in range(4)]
        r_hb4 = [Res("hb%d" % i) for i in range(4)]
        xn2 = [self.sb([128, D], BF16, "xn") for _ in range(2)]
        r_xn2 = [Res("xn%d" % i) for i in range(2)]
        st = [[self.sb([128, 1], F32, "st") for _ in range(3)] for _ in range(2)]
        r_st = [Res("st%d" % i) for i in range(2)]
        xnT = [self.sb([128, NKC, T], BF16, "xnT") for _ in range(2)]
        r_xnT = [Res("xnT%d" % i) for i in range(2)]
        aT = self.sb([128, NFC, T], BF16, "aT")
        r_aT = [Res("aT%d" % j) for j in range(NFC)]
        sg2 = self.sb([128, 2, T], F32, "sg")
        sg = [sg2[:, 0, :], sg2[:, 1, :]]
        r_sg = [Res("sg%d" % i) for i in range(2)]
        if oproj is not None:
            Wo, r_wo = oproj
            OTs = [self.sb([128, NKC, T], BF16, "OTs") for _ in range(2)]
            r_ots = [Res("ots0"), Res("ots1")]
            otv = self.OT.rearrange("(k p) s -> p k s", p=128)
        if final:
            fss = [self.sb([128, 1], F32, "fss") for _ in range(2)]
            fjunk = sg2[:].rearrange("p a b -> p (a b)").bitcast(BF16)
        GB, UB, YB, TRB = (0, 1), (2, 3), (4, 5), (6, 7)
        yc = 0

        def hslot(t, ts):
            return (2 * t + ts) % 4

        def loads(t):
            for ts in range(2):
                k = hslot(t, ts)
                r0 = t * T + ts * 128
                self.dma("sp", hb4[k][:], h_in[r0:r0 + 128, :], [self.r_h], [r_hb4[k]])
            if oproj is not None:
                self.dma("sp", OTs[t % 2][:], otv[:, :, t * T:(t + 1) * T], [self.r_ot], [r_ots[t % 2]])

        def oproj_tile(t):
            nonlocal yc
            if oproj is None:
                return
            ot, rot = OTs[t % 2], r_ots[t % 2]
            for ts in range(2):
                k = hslot(t, ts)
                hb, rhb = hb4[k], r_hb4[k]
                for half in range(2):
                    ybk = YB[yc % 2]
                    yc += 1
                    yps = self.psum[ybk][:, :]

                    def f_mm(e, ts=ts, half=half, yps=yps, ot=ot):
                        ins = None
                        for kc in range(NKC):
                            ins = e.matmul(yps, lhsT=ot[:, kc, ts * 128:(ts + 1) * 128],
                                           rhs=Wo[:, kc, half * 512:(half + 1) * 512],
                                           start=(kc == 0), stop=(kc == NKC - 1))
                        return ins
                    self.op("pe", f_mm, [rot, r_wo], [self.pres[ybk]])
                    self.op("dve", lambda e, hb=hb, half=half, yps=yps: e.tensor_tensor(
                        out=hb[:, half * 512:(half + 1) * 512], in0=yps, in1=hb[:, half * 512:(half + 1) * 512],
                        op=ALU.add), [self.pres[ybk], rhb], [rhb])

        def norm_chain(t):
            for ts in range(2):
                k = hslot(t, ts)
                hb, rhb = hb4[k], r_hb4[k]
                xn, rxn = xn2[ts], r_xn2[ts]
                ss, ms, rstd = st[ts]
                rst = r_st[ts]
                self.op("act", lambda e, xn=xn, hb=hb, ss=ss: e.activation(out=xn[:], in_=hb[:], func=AF.Square,
                                                                          accum_out=ss[:]), [rhb], [rxn, rst])

                self.op("pool", lambda e, ss=ss, ms=ms: e.tensor_scalar(out=ms[:], in0=ss[:], scalar1=1.0 / D, scalar2=RMS_EPS,
                                                                        op0=ALU.mult, op1=ALU.add), [rst], [rst])
                self.op("pool", lambda e, ms=ms, rstd=rstd: e.tensor_tensor(out=rstd[:], in0=ms[:], in1=self.neghalf[:],
                                                                            op=ALU.pow), [rst, self.r_const], [rst])
                self.op("dve", lambda e, xn=xn, hb=hb, rstd=rstd: e.scalar_tensor_tensor(
                    out=xn[:], in0=hb[:], scalar=rstd[:], in1=gb[:], op0=ALU.mult, op1=ALU.mult),
                    [rhb, rst, r_gb], [rxn])

        def norm_T_tile(t):
            for ts in range(2):
                xn, rxn = xn2[ts], r_xn2[ts]
                ps = self.psum[TRB[ts]][:].bitcast(BF16).rearrange("p (k t) -> p k t", k=8)
                pr = self.pres[TRB[ts]]

                def f_tr(e, xn=xn, ps=ps):
                    ins = None
                    for kc in range(NKC):
                        ins = e.transpose(out=ps[:, kc, :], in_=xn[:, kc * 128:(kc + 1) * 128], identity=self.identb[:])
                    return ins
                self.op("pe", f_tr, [rxn, self.r_const], [pr])
                dst = xnT[t % 2][:, :, ts * 128:(ts + 1) * 128]
                self.op("act", lambda e, dst=dst, ps=ps: e.copy(out=dst, in_=ps), [pr], [r_xnT[t % 2]])

        loads(0)
        oproj_tile(0)
        norm_chain(0)
        norm_T_tile(0)
        pair = 0
        for t in range(NT):
            xt = xnT[t % 2]
            rxt = r_xnT[t % 2]
            if t + 1 < NT:
                loads(t + 1)
            for j in range(NFC):
                gbk, ubk = GB[pair % 2], UB[pair % 2]
                sgt, rsg = sg[pair % 2], r_sg[pair % 2]
                pair += 1
                gps = self.psum[gbk][:, 0:T]
                ups = self.psum[ubk][:, 0:T]

                def f_mm1(e, j=j, gps=gps, ups=ups, xt=xt):
                    for kc in range(NKC):
                        e.matmul(gps, lhsT=Win[:, kc, j * 128:(j + 1) * 128], rhs=xt[:, kc, :],
                                 start=(kc == 0), stop=(kc == NKC - 1))
                    ins = None
                    for kc in range(NKC):
                        ins = e.matmul(ups, lhsT=Win[:, kc, FF + j * 128:FF + (j + 1) * 128], rhs=xt[:, kc, :],
                                       start=(kc == 0), stop=(kc == NKC - 1))
                    return ins
                self.op("pe", f_mm1, [rxt] + r_win, [self.pres[gbk], self.pres[ubk]])
                self.op("act", lambda e, sgt=sgt, gps=gps: e.activation(out=sgt, in_=gps, func=AF.Silu),
                        [self.pres[gbk]], [rsg])
                self.op("dve", lambda e, j=j, sgt=sgt, ups=ups: e.tensor_tensor(out=aT[:, j, :], in0=sgt, in1=ups,
                                                                                 op=ALU.mult),
                        [rsg, self.pres[ubk]], [r_aT[j]])
                if j == 3 and t + 1 < NT:
                    oproj_tile(t + 1)
                    norm_chain(t + 1)
            if t + 1 < NT:
                norm_T_tile(t + 1)
            for ts in range(2):
                k = hslot(t, ts)
                hb, rhb = hb4[k], r_hb4[k]
                fs = fss[ts] if final else None
                r0 = t * T + ts * 128
                for half in range(2):
                    ybk = YB[yc % 2]
                    yc += 1
                    yps = self.psum[ybk][:, :]

                    def f_mm2(e, ts=ts, half=half, yps=yps):
                        ins = None
                        for kc in range(NFC):
                            ins = e.matmul(yps, lhsT=aT[:, kc, ts * 128:(ts + 1) * 128],
                                           rhs=Wout[:, kc, half * 512:(half + 1) * 512],
                                           start=(kc == 0), stop=(kc == NFC - 1))
                        return ins
                    self.op("pe", f_mm2, r_aT + [r_wout], [self.pres[ybk]])
                    self.op("dve", lambda e, hb=hb, half=half, yps=yps: e.scalar_tensor_tensor(
                        out=hb[:, half * 512:(half + 1) * 512], in0=yps, scalar=0.5,
                        in1=hb[:, half * 512:(half + 1) * 512], op0=ALU.mult, op1=ALU.add),
                        [self.pres[ybk], rhb], [rhb])
                if not final:
                    self.dma("sp", self.h[r0:r0 + 128, :], hb[:], [rhb], [self.r_h])
                else:
                    self.op("act", lambda e, hb=hb, fs=fs: e.activation(out=fjunk, in_=hb[:], func=AF.Square,
                                                                      accum_out=fs[:]), [rhb] + r_sg, [rhb] + r_sg)

                    self.op("pool", lambda e, fs=fs: e.tensor_scalar(out=fs[:], in0=fs[:], scalar1=1.0 / D, scalar2=RMS_EPS,
                                                                     op0=ALU.mult, op1=ALU.add), [rhb], [rhb])
                    self.op("pool", lambda e, fs=fs: e.tensor_tensor(out=fs[:], in0=fs[:], in1=self.neghalf[:], op=ALU.pow),
                            [rhb, self.r_const], [rhb])
                    self.op("dve", lambda e, hb=hb, fs=fs: e.scalar_tensor_tensor(
                        out=hb[:], in0=hb[:], scalar=fs[:], in1=gfb[:], op0=ALU.mult, op1=ALU.mult),
                        [rhb, r_gfb], [rhb])
                    self.dma("sp", self.out[r0:r0 + 128, :], hb[:], [rhb], [self.r_h])
        self.s.barrier()
        self.release(m0)

    def setup_attn_consts(self):
        self.neglam = self.sb([128, 1], F32, "neglam")
        self.gsb = self.sb([128, 128], F32, "gsb")
        m0 = self.mark()
        self.DE = [self.sb([128, 16, 128], BF16, "DE%d" % ty) for ty in range(2)]
        self.r_DE = Res("DE")
        self.r_DEd = Res("DEd")
        rb = self.sb([32, 16], F32, "rb")
        oh = self.sb([32, 256], F32, "oh")
        r_rb, r_oh = Res("rb"), Res("oh")
        self.dma("sp", rb[:], self.rel_bias, [], [r_rb])
        self.dma("sp", oh[:], self.c_oh, [], [r_oh])
        gps = self.psum[0][0:16, 0:256]
        self.op("pe", lambda e: e.matmul(gps, lhsT=rb[:], rhs=oh[:], start=True, stop=True), [r_rb, r_oh], [self.pres[0]])
        cv = self.sb([16, 1], F32, "cv")
        gs = self.sb([16, 256], F32, "gs")
        egf = self.sb([16, 384], F32, "egf")
        r_cv, r_gs, r_egf = Res("cv"), Res("gs"), Res("egf")
        self.op("dve", lambda e: e.tensor_copy(out=cv[:], in_=gps[:, 255:256]), [self.pres[0]], [r_cv])
        self.op("dve", lambda e: e.tensor_scalar(out=gs[:], in0=gps, scalar1=cv[:], scalar2=None, op0=ALU.subtract),
                [self.pres[0], r_cv], [r_gs])
        self.op("pool", lambda e: e.memset(egf[:, 0:128], 0.0), [], [r_egf])
        self.op("act", lambda e: e.activation(out=egf[:, 128:384], in_=gs[:], func=AF.Exp), [r_gs, r_egf], [r_egf])
        r_egd = Res("egd")
        self.dma("sp", self.EGd, egf[:], [r_egf], [r_egd])
        anti = self.sb([128, 128], F32, "anti")
        r_anti = Res("anti")
        self.dma("sp", anti[:], self.c_anti, [], [r_anti])
        hd = [self.sb([128, 16, 128], F32, "hd") for _ in range(2)]
        r_hd = [Res("hd0"), Res("hd1")]
        for ty in range(2):
            src = bass.AP(tensor=self.EGd.tensor, offset=1 + 128 * ty, ap=[[1, 128], [384, 16], [1, 128]])
            self.dma("sp", hd[ty][:], src, [r_egd], [r_hd[ty]])
        for ty in range(2):
            hv = hd[ty][:].rearrange("p m q -> p (m q)")
            dv = self.DE[ty][:].rearrange("p m q -> p (m q)")
            for c in range(4):
                bk = 1 + (ty * 4 + c) % 4
                ps = self.psum[bk][:, :]
                self.op("pe", lambda e, ps=ps, hv=hv, c=c: e.matmul(ps, lhsT=anti[:], rhs=hv[:, c * 512:(c + 1) * 512],
                                                                   start=True, stop=True),
                        [r_anti, r_hd[ty]], [self.pres[bk]])
                self.op("dve", lambda e, ps=ps, dv=dv, c=c: e.tensor_copy(out=dv[:, c * 512:(c + 1) * 512], in_=ps),
                        [self.pres[bk]], [self.r_DE])
        lp = self.sb([128, 256], F32, "lp")
        r_lp = Res("lp")
        self.dma("sp", lp[:], self.diff_lambda.partition_broadcast(128), [], [r_lp])
        lt = self.sb([128, 2, 64], F32, "lt")
        s12 = self.sb([128, 2], F32, "s12")
        e12 = self.sb([128, 2], F32, "e12")
        r_lt, r_s12, r_e12 = Res("lt"), Res("s12"), Res("e12")
        self.r_lam = Res("lam")
        lv = lp[:].rearrange("p (a b d) -> p a b d", a=2, b=2)
        self.op("dve", lambda e: e.tensor_tensor(out=lt[:], in0=lv[:, :, 0, :], in1=lv[:, :, 1, :], op=ALU.mult), [r_lp], [r_lt])
        self.op("dve", lambda e: e.tensor_reduce(out=s12[:], in_=lt[:], axis=AX.X, op=ALU.add), [r_lt], [r_s12])
        self.op("act", lambda e: e.activation(out=e12[:], in_=s12[:], func=AF.Exp), [r_s12], [r_e12])

        self.op("pool", lambda e: e.tensor_tensor(out=self.neglam[:], in0=e12[:, 1:2], in1=e12[:, 0:1], op=ALU.subtract),
                [r_e12], [self.r_lam])
        self.op("pool", lambda e: e.tensor_scalar(out=self.neglam[:], in0=self.neglam[:], scalar1=-0.2, scalar2=None,
                                                  op0=ALU.add), [self.r_lam], [self.r_lam])
        r_g0 = Res("g0")
        self.dma("sp", self.gsb[:], self.diff_subln_g.partition_broadcast(128), [], [r_g0])
        self.op("pool", lambda e: e.tensor_scalar(out=self.gsb[:], in0=self.gsb[:], scalar1=0.8, scalar2=None, op0=ALU.mult),
                [r_g0], [self.r_lam])
        for ty in range(2):
            self.dma("sp", self.DEd[ty], self.DE[ty][:], [self.r_DE], [self.r_DEd])
        self.s.barrier()
        self.release(m0)

    def qkv_phase(self, l, mode):
        S = self.S
        T = 512
        NT = S // T
        m0 = self.mark()
        w = self.diff_w_qkv if mode == "diff" else self.moba_w_qkv
        W = self.sb([128, NKC, 3 * D], BF16, "Wqkv")
        r_w = [Res("wqkv%d" % k) for k in range(NKC)]
        for kc in range(NKC):
            self.dma("pool", W[:, kc, :], w[kc * 128:(kc + 1) * 128, :], [], [r_w[kc]], max_dma_last_dim=8192)
        gb = self.sb([128, D], F32, "gb")
        r_gb = Res("gb")
        self.load_gain(gb, r_gb, self.norm_g[l, 1:2, :])
        nb = self.norm_bufs(4)
        hnT = [self.sb([128, NKC, T], BF16, "hnT") for _ in range(2)]
        r_hnT = [Res("hnT%d" % i) for i in range(2)]
        qst = [self.sb([128, T], BF16, "qst") for _ in range(4)]
        r_qst = [Res("qst%d" % i) for i in range(4)]
        vst = [self.sb([128, 1040], BF16, "vst") for _ in range(2)]
        r_vst = [Res("vst%d" % i) for i in range(2)]
        for i in range(2):
            self.op("pool", lambda e, i=i: e.memset(vst[i][:], 1.0), [], [r_vst[i]])
        QKB, VB, TRB = (0, 1, 2, 3), (4, 5), (6, 7)

        def pre(t):
            return [self.norm_pre(nb, self.h[t * T + ts * 128: t * T + (ts + 1) * 128, :], [self.r_h], gb, r_gb)
                    for ts in range(4)]

        def post(t, slots):
            for ts, i in enumerate(slots):
                self.norm_T(nb, i, TRB[ts % 2], hnT[t % 2][:, :, ts * 128:(ts + 1) * 128], r_hnT[t % 2],
                            copy_eng=("act" if ts % 2 == 0 else "dve"))
        slots = pre(0)
        post(0, slots)
        qc = 0
        vc = 0
        for t in range(NT):
            ht, rht = hnT[t % 2], r_hnT[t % 2]
            if t + 1 < NT:
                nslots = pre(t + 1)
            for c in range(16):
                bk = QKB[qc % 4]
                st, rst = qst[qc % 4], r_qst[qc % 4]
                ps = self.psum[bk][:, :]

                def f_mm(e, c=c, ps=ps, ht=ht):
                    ins = None
                    for kc in range(NKC):
                        ins = e.matmul(ps, lhsT=W[:, kc, c * 128:(c + 1) * 128], rhs=ht[:, kc, :],
                                       start=(kc == 0), stop=(kc == NKC - 1))
                    return ins
                self.op("pe", f_mm, [rht] + r_w, [self.pres[bk]])
                if qc % 2 == 0:
                    self.op("act", lambda e, st=st, ps=ps: e.copy(out=st[:], in_=ps), [self.pres[bk]], [rst])
                else:
                    self.op("dve", lambda e, st=st, ps=ps: e.tensor_copy(out=st[:], in_=ps), [self.pres[bk]], [rst])
                dst = self.QT if c < 8 else self.KT
                cc = c % 8
                self.dma("sp", dst[cc * 128:(cc + 1) * 128, t * T:(t + 1) * T], st[:], [rst], [self.r_qkv])
                qc += 1
            if t + 1 < NT:
                post(t + 1, nslots)
            for ts in range(4):
                vt, rvt = vst[vc % 2], r_vst[vc % 2]
                vc += 1
                for half in range(2):
                    bk = VB[half]
                    ps = self.psum[bk][:, :]

                    def f_mv(e, ts=ts, half=half, ps=ps, ht=ht):
                        ins = None
                        for kc in range(NKC):
                            ins = e.matmul(ps, lhsT=ht[:, kc, ts * 128:(ts + 1) * 128],
                                           rhs=W[:, kc, 2 * D + half * 512: 2 * D + (half + 1) * 512],
                                           start=(kc == 0), stop=(kc == NKC - 1))
                        return ins
                    self.op("pe", f_mv, [rht] + r_w, [self.pres[bk]])
                    if mode == "diff":
                        o = vt[:, 0:1032].rearrange("p (h c) -> p h c", c=129)[:, 4 * half:4 * half + 4, 0:128]
                        i_ = ps.rearrange("p (h c) -> p h c", c=128)
                    else:
                        o = vt[:, 0:1040].rearrange("p (h c) -> p h c", c=65)[:, 8 * half:8 * half + 8, 0:64]
                        i_ = ps.rearrange("p (h c) -> p h c", c=64)
                    if half == 0:
                        self.op("act", lambda e, o=o, i_=i_: e.copy(out=o, in_=i_), [self.pres[bk]], [rvt])
                    else:
                        self.op("dve", lambda e, o=o, i_=i_: e.tensor_copy(out=o, in_=i_), [self.pres[bk]], [rvt])
                r0 = t * T + ts * 128
                self.dma("sp", self.VA[r0:r0 + 128, :], vt[:], [rvt], [self.r_qkv])
        self.s.barrier()
        self.release(m0)

    def attn_phase(self, mode):
        S = self.S
        NQT = S // 512
        NKT = S // 128
        m0 = self.mark()
        diff = (mode == "diff")
        VAs = self.sb([128, NKT, 1040], BF16, "VAs")
        r_va = Res("va")
        vav = self.VA.rearrange("(k p) c -> p k c", p=128)
        step = max(1, NKT // 4)
        for k0 in range(0, NKT, step):
            self.dma("sp", VAs[:, k0:k0 + step, :], vav[:, k0:k0 + step, :], [self.r_qkv], [r_va])
        nheads = 8 if diff else 16
        nmaps = 2 if diff else 1
        DE = [self.sb([128, 16, 128], BF16, "DE%d" % ty) for ty in range(2)]
        r_DE = Res("DEl")
        for ty in range(2):
            self.dma("sp", DE[ty][:], self.DEd[ty], [self.r_DEd], [r_DE])
        Qs = [self.sb([128, S], BF16, "Qs") for _ in range(2)]
        r_q = [Res("q0"), Res("q1")]
        r_k = [Res("k0"), Res("k1")]
        r_cst = Res("acst")
        if diff:
            KA = [self.sb([128, S], BF16, "KA") for _ in range(2)]
            KB = [self.sb([128, S], BF16, "KB") for _ in range(2)]
            for b in range(2):
                self.op("pool", lambda e, b=b: e.memset(KA[b][64:128, :], 0.0), [], [r_k[b]])
                self.op("pool", lambda e, b=b: e.memset(KB[b][0:64, :], 0.0), [], [r_k[b]])
            ones_b = self.sb([128, 128], BF16, "ones_b")
            ones_f = self.sb([128, 128], F32, "ones_f")
            gcol = self.sb([128, 1], F32, "gcol")
            self.op("pool", lambda e: e.memset(ones_b[:], 1.0), [], [r_cst])
            self.op("pool", lambda e: e.memset(ones_f[:], 1.0), [], [r_cst])
            r_gc = Res("gc")
            self.dma("sp", gcol[:], self.diff_subln_g.rearrange("o v -> v o"), [], [r_gc])
            self.op("pool", lambda e: e.tensor_scalar(out=gcol[:], in0=gcol[:], scalar1=0.8, scalar2=None, op0=ALU.mult),
                    [r_gc], [r_cst])
        else:
            Ks = [self.sb([128, S], BF16, "Ks") for _ in range(2)]
            for b in range(2):
                self.op("pool", lambda e, b=b: e.memset(Qs[b][64:128, :], 0.0), [], [r_q[b]])
                self.op("pool", lambda e, b=b: e.memset(Ks[b][64:128, :], 0.0), [], [r_k[b]])
                self.dma("pool", Ks[b][64:80, :], self.c_ind, [], [r_k[b]], max_dma_last_dim=8192)
            sel_f = self.sb([65, 64], F32, "sel_f")
            self.op("pool", lambda e: e.memset(sel_f[:], 0.0), [], [r_cst])
            self.op("pool", lambda e: e.memset(sel_f[64:65, :], 1.0), [], [r_cst])
        NPT = 8 if diff else 6
        smc = 0
        if diff:
            Psm = [self.sb([128, 512], BF16, "Psm") for _ in range(4)]
            r_psm = [Res("psm%d" % i) for i in range(4)]
        Pt = [self.sb([128, 512], BF16, "Pt") for _ in range(NPT)]
        r_pt = [Res("pt%d" % i) for i in range(NPT)]
        ost = [self.sb([128, 512], BF16, "ost") for _ in range(2)]
        r_ost = [Res("ost0"), Res("ost1")]
        if diff:
            f32t = lambda n: self.sb([128, 512], F32, n)
            e_s0, e_s1, e_t0, e_t1, e_o, e_sq, e_l, e_rs, e_c = [f32t("e%d" % i) for i in range(9)]
            r_e = Res("ep")
            r_e2 = Res("ep2")
            r_es = Res("es")
            r_ec = Res("ec")
        else:
            oa = [self.sb([65, 512], F32, "oa") for _ in range(2)]
            r_oa = [Res("oa0"), Res("oa1")]
            rcp = self.sb([64, 512], F32, "rcp")
            r_rcp = Res("rcp")
        SB = (0, 1, 2) if diff else (0, 1, 2, 6, 7)
        NSB = len(SB)
        sc = 0
        pc = 0
        LOOK = 2 if diff else 4

        def load_head(hh):
            b = hh % 2
            if diff:
                self.dma("sp", Qs[b][:], self.QT[hh * 128:(hh + 1) * 128, :], [self.r_qkv], [r_q[b]])
                self.dma("sp", KA[b][0:64, :], self.KT[hh * 128:hh * 128 + 64, :], [self.r_qkv], [r_k[b]])
                self.dma("sp", KB[b][64:128, :], self.KT[hh * 128 + 64:hh * 128 + 128, :], [self.r_qkv], [r_k[b]])
            else:
                self.dma("sp", Qs[b][0:64, :], self.QT[hh * 64:(hh + 1) * 64, :], [self.r_qkv], [r_q[b]])
                self.dma("sp", Qs[b][64:80, :], self.PT[hh], [self.r_pt], [r_q[b]])
                self.dma("sp", Ks[b][0:64, :], self.KT[hh * 64:(hh + 1) * 64, :], [self.r_qkv], [r_k[b]])

        load_head(0)
        deferred = []
        it = 0
        for hh in range(nheads):
            hb = hh % 2
            if hh + 1 < nheads:
                load_head(hh + 1)
            Qh = Qs[hb]
            for j in range(NQT):
                nkt = 4 * j + 4
                if diff:
                    obank = (3, 4)
                    sbank = (5, 6)
                    vcol, VM = hh * 129, 128
                    kmats = (KA[hb], KB[hb])
                else:
                    obank = (3 + it % 2,)
                    vcol, VM = hh * 65, 65
                    kmats = (Ks[hb],)
                units = [(i, kt) for kt in range(nkt) for i in range(nmaps)]
                pend = []

                def do_qk(i, kt, j=j, hh=hh, hb=hb, Qh=Qh, kmats=kmats, pend=pend):
                    nonlocal sc, pc
                    bk = SB[sc % NSB]
                    sc += 1
                    pt, rpt = Pt[pc % NPT], r_pt[pc % NPT]
                    pc += 1
                    m = (2 * hh + i) if diff else hh
                    s0 = max(0, kt - 4 * j)
                    c0 = s0 * 128
                    ps = self.psum[bk][:, c0:512]
                    lhs = kmats[i][:, kt * 128:(kt + 1) * 128]
                    rhs = Qh[:, j * 512 + c0:(j + 1) * 512]
                    self.op("pe", lambda e, ps=ps, lhs=lhs, rhs=rhs: e.matmul(ps, lhsT=lhs, rhs=rhs, start=True, stop=True),
                            [r_q[hb], r_k[hb]], [self.pres[bk]])
                    self.op("act", lambda e, ps=ps, pt=pt, c0=c0: e.activation(out=pt[:, c0:512], in_=ps, func=AF.Exp,
                                                                           scale=0.125),
                            [self.pres[bk]], [rpt])
                    for s in range(s0, 4):
                        qi = 4 * j + s
                        if kt == qi or kt == qi - 1:
                            ty = 0 if kt == qi else 1
                            de = DE[ty][:, m, :]
                            self.op("dve", lambda e, pt=pt, s=s, de=de: e.tensor_tensor(
                                out=pt[:, s * 128:(s + 1) * 128], in0=pt[:, s * 128:(s + 1) * 128],
                                in1=de, op=ALU.mult), [rpt, r_DE], [rpt])
                    pend.append((i, kt, pt, rpt, c0))

                sfirst = [True, True]
                pairbuf = [None, None]
                spend = []

                def do_pv(j=j, obank=obank, vcol=vcol, VM=VM, pend=pend, nkt=nkt, sfirst=sfirst, pairbuf=pairbuf, spend=spend):
                    i, kt, pt, rpt, c0 = pend.pop(0)
                    ob = obank[i]
                    vst_ = VAs[:, kt, vcol:vcol + VM]
                    oacc = self.psum[ob][0:VM, c0:512]
                    if diff:
                        nonlocal smc
                        sb_ = sbank[i]
                        sacc = self.psum[sb_][:, c0:512]
                        while spend:
                            spend.pop(0)()
                        self.op("pe", lambda e, kt=kt, pt=pt, c0=c0, oacc=oacc, vst_=vst_: e.matmul(
                            oacc, lhsT=vst_, rhs=pt[:, c0:512], start=(kt == 0), stop=(kt == nkt - 1)),
                            [rpt, r_va], [self.pres[ob]])
                        if kt % 2 == 0 and kt + 1 <= 4 * j:
                            pairbuf[i] = (pt, rpt)
                        elif kt % 2 == 1 and kt <= 4 * j:
                            ptA, rptA = pairbuf[i]
                            sm, rsm = Psm[smc % 4], r_psm[smc % 4]
                            smc += 1
                            self.op("dve", lambda e, sm=sm, ptA=ptA, pt=pt: e.tensor_tensor(out=sm[:], in0=ptA[:], in1=pt[:],
                                                                                             op=ALU.add), [rptA, rpt], [rsm])
                            first = sfirst[i]
                            sfirst[i] = False

                            def issue(sm=sm, rsm=rsm, first=first, sb_=sb_):
                                self.op("pe", lambda e: e.matmul(self.psum[sb_][:, :], lhsT=ones_b[:], rhs=sm[:], start=first,
                                                                 stop=False), [rsm, r_cst], [self.pres[sb_]])
                            spend.append(issue)
                        else:
                            first = sfirst[i]
                            sfirst[i] = False
                            self.op("pe", lambda e, kt=kt, pt=pt, c0=c0, sacc=sacc, first=first: e.matmul(
                                sacc, lhsT=ones_b[:], rhs=pt[:, c0:512], start=first, stop=(kt == nkt - 1)),
                                [rpt, r_cst], [self.pres[sb_]])
                    else:
                        self.op("pe", lambda e, kt=kt, pt=pt, c0=c0, oacc=oacc, vst_=vst_: e.matmul(
                            oacc, lhsT=vst_, rhs=pt[:, c0:512], start=(kt == 0), stop=(kt == nkt - 1)),
                            [rpt, r_va], [self.pres[ob]])

                nun = len(units)
                for idx in range(nun + LOOK):
                    if idx < nun:
                        do_qk(*units[idx])
                    if idx >= LOOK:
                        do_pv()
                    while deferred and deferred[0][0] <= idx:
                        deferred.pop(0)[1]()
                while spend:
                    spend.pop(0)()
                while deferred:
                    deferred.pop(0)[1]()
                k = it % 2
                if diff:
                    oa_, ob_, sa_, sb2_ = (self.psum[b][:, :] for b in (3, 4, 5, 6))
                    self.op("dve", lambda e, sa_=sa_: e.tensor_scalar(out=e_s0[:], in0=sa_, scalar1=2.0 ** -10, scalar2=None,
                                                                     op0=ALU.mult), [self.pres[5]], [r_es])
                    self.op("dve", lambda e, sb2_=sb2_: e.tensor_scalar(out=e_s1[:], in0=sb2_, scalar1=2.0 ** -10, scalar2=None,
                                                                       op0=ALU.mult), [self.pres[6]], [r_es])
                    self.op("dve", lambda e, oa_=oa_: e.tensor_tensor(out=e_t0[:], in0=oa_, in1=e_s1[:], op=ALU.mult),
                            [self.pres[3], r_es], [r_e])
                    self.op("dve", lambda e, ob_=ob_: e.tensor_tensor(out=e_t1[:], in0=ob_, in1=e_s0[:], op=ALU.mult),
                            [self.pres[4], r_es], [r_e])
                    self.op("dve", lambda e: e.scalar_tensor_tensor(out=e_o[:], in0=e_t1[:], scalar=self.neglam[:],
                                                                    in1=e_t0[:], op0=ALU.mult, op1=ALU.add),
                            [r_e, self.r_lam], [r_e])
                    self.op("pool", lambda e: e.tensor_tensor(out=e_c[:], in0=e_s0[:], in1=e_s1[:], op=ALU.mult), [r_es], [r_ec])
                    self.op("pool", lambda e: e.tensor_tensor(out=e_c[:], in0=e_c[:], in1=e_c[:], op=ALU.mult), [r_ec], [r_ec])
                    self.op("pool", lambda e: e.tensor_scalar(out=e_c[:], in0=e_c[:], scalar1=SUBLN_EPS, scalar2=1.0,
                                                              op0=ALU.mult, op1=ALU.mult), [r_ec], [r_ec])
                    self.op("dve", lambda e: e.tensor_tensor(out=e_sq[:], in0=e_o[:], in1=e_o[:], op=ALU.mult), [r_e], [r_e])

                    def st1():
                        self.op("pe", lambda e: e.matmul(self.psum[7][:, :], lhsT=ones_f[:], rhs=e_sq[:], start=True, stop=True),
                                [r_e, r_cst], [self.pres[7]])

                    def st2(hh=hh, j=j, k=k):
                        self.op("dve", lambda e: e.scalar_tensor_tensor(out=e_l[:], in0=self.psum[7][:, :], scalar=1.0 / 128,
                                                                        in1=e_c[:], op0=ALU.mult, op1=ALU.add),
                                [self.pres[7], r_ec], [r_e2])
                        self.op("act", lambda e: e.activation(out=e_l[:], in_=e_l[:], func=AF.Ln), [r_e2], [r_e2])
                        self.op("act", lambda e: e.activation(out=e_rs[:], in_=e_l[:], func=AF.Exp, scale=-0.5), [r_e2], [r_e2])
                        self.op("dve", lambda e, k=k: e.scalar_tensor_tensor(out=ost[k][:], in0=e_o[:], scalar=gcol[:],
                                                                              in1=e_rs[:], op0=ALU.mult, op1=ALU.mult),
                                [r_e, r_e2, r_cst], [r_ost[k], r_e])
                        self.dma("sp", self.OT[hh * 128:(hh + 1) * 128, j * 512:(j + 1) * 512], ost[k][:], [r_ost[k]],
                                 [self.r_ot])
                    deferred.append((5, st1))
                    deferred.append((10, st2))
                else:
                    ob = obank[0]
                    oak, roak = oa[k], r_oa[k]
                    sbk = 5
                    self.op("dve", lambda e, ob=ob, oak=oak: e.tensor_copy(out=oak[:], in_=self.psum[ob][0:65, :]),
                            [self.pres[ob]], [roak])

                    def st1(oak=oak, roak=roak, sbk=sbk):
                        self.op("pe", lambda e: e.matmul(self.psum[sbk][0:64, :], lhsT=sel_f[:], rhs=oak[:], start=True, stop=True),
                                [roak, r_cst], [self.pres[sbk]])

                    def st2(hh=hh, j=j, k=k, oak=oak, roak=roak, sbk=sbk):
                        self.op("dve", lambda e: e.reciprocal(out=rcp[:], in_=self.psum[sbk][0:64, :]), [self.pres[sbk]], [r_rcp])
                        self.op("dve", lambda e: e.tensor_tensor(out=ost[k][0:64, :], in0=oak[0:64, :], in1=rcp[:], op=ALU.mult),
                                [roak, r_rcp], [r_ost[k]])
                        self.dma("sp", self.OT[hh * 64:(hh + 1) * 64, j * 512:(j + 1) * 512], ost[k][0:64, :], [r_ost[k]],
                                 [self.r_ot])
                    deferred.append((4, st1))
                    deferred.append((8, st2))
                it += 1
        while deferred:
            deferred.pop(0)[1]()
        self.s.barrier()
        self.release(m0)

    def gate_phase(self):
        S = self.S
        NB = S // 256
        NQ = S // 128
        m0 = self.mark()
        QTa = self.sb([128, 8, S], BF16, "QTa")
        r_qta = Res("qta")
        qv = self.QT.rearrange("(c p) s -> p c s", p=128)
        for c0 in range(0, 8, 2):
            self.dma("sp", QTa[:, c0:c0 + 2, :], qv[:, c0:c0 + 2, :], [self.r_qkv], [r_qta])
        KTb = [self.sb([128, S], BF16, "KTb") for _ in range(2)]
        r_ktb = [Res("ktb0"), Res("ktb1")]
        KM = self.sb([128, 8, 16], F32, "KM")
        KMb = self.sb([128, 8, 16], BF16, "KMb")
        r_km, r_kmb = Res("km"), Res("kmb")
        self.op("pool", lambda e: e.memset(KM[:], 0.0), [], [r_km])
        for c in range(8):
            kb, rkb = KTb[c % 2], r_ktb[c % 2]
            self.dma("sp", kb[:], self.KT[c * 128:(c + 1) * 128, :], [self.r_qkv], [rkb])
            self.op("dve", lambda e, c=c, kb=kb: e.tensor_reduce(out=KM[:, c, 0:NB],
                                                                in_=kb[:].rearrange("p (n l) -> p n l", l=256),
                                                                axis=AX.X, op=ALU.add), [rkb, r_km], [r_km])
        self.op("dve", lambda e: e.tensor_copy(out=KMb[:], in_=KM[:]), [r_km], [r_kmb])
        gs = [self.sb([128, 16, 16], F32, "gs") for _ in range(2)]
        r_gs = [Res("gs0"), Res("gs1")]
        pen = [self.sb([128, 16, 16], BF16, "pen") for _ in range(2)]
        r_pen = [Res("pen0"), Res("pen1")]
        mx = self.sb([128, 16, 8], F32, "mx")
        r_mx = Res("mx")
        pst = [self.sb([16, 16, 512], BF16, "pst") for _ in range(2)]
        r_pst = [Res("pst0"), Res("pst1")]
        for b in range(2):
            self.op("pool", lambda e, b=b: e.memset(gs[b][:], -1e30), [], [r_gs[b]])
        ptv = self.PT.rearrange("(c r) n s -> n r c s", r=2)
        for qi in range(NQ):
            qblk = qi // 2
            ne = qblk
            b = qi % 2
            gb0, gb1 = 2 * b, 2 * b + 1
            g, rg = gs[b], r_gs[b]
            pn, rpn = pen[b], r_pen[b]
            ps0 = self.psum[gb0][:, 0:128]
            ps1 = self.psum[gb1][:, 0:128]

            def f_g(e, qi=qi, ps0=ps0, ps1=ps1):
                ins = None
                for par in range(2):
                    ps = ps0 if par == 0 else ps1
                    p0 = 64 * par
                    for c in range(8):
                        ins = e.matmul(ps[:, c * 16:(c + 1) * 16], lhsT=QTa[p0:p0 + 64, c, qi * 128:(qi + 1) * 128],
                                       rhs=KMb[p0:p0 + 64, c, :], start=True, stop=True, skip_group_check=True)
                return ins
            self.op("pool", lambda e, pn=pn: e.memset(pn[:], NEG_BIG), [], [rpn])
            self.op("pool", lambda e, pn=pn, qblk=qblk: e.memset(pn[:, :, qblk:qblk + 1], 0.0), [], [rpn])
            if 1 <= ne <= 3:
                self.op("pool", lambda e, pn=pn, ne=ne: e.memset(pn[:, :, 0:ne], 0.0), [], [rpn])
            elif ne > 3:
                self.op("pe", f_g, [r_qta, r_kmb], [self.pres[gb0], self.pres[gb1]])
                for par in range(2):
                    psv = (ps0 if par == 0 else ps1).rearrange("p (h n) -> p h n", n=16)
                    self.op("dve", lambda e, g=g, psv=psv, ne=ne, par=par: e.tensor_copy(
                        out=g[:, 8 * par:8 * par + 8, 0:ne], in_=psv[:, :, 0:ne]),
                        [self.pres[gb0 + par]], [rg])
                for h in range(16):
                    self.op("dve", lambda e, g=g, h=h: e.max(out=mx[:, h, :], in_=g[:, h, :]), [rg], [r_mx])
                for h in range(16):
                    self.op("dve", lambda e, g=g, h=h, pn=pn, ne=ne: e.tensor_scalar(
                        out=pn[:, h, 0:ne], in0=g[:, h, 0:ne], scalar1=mx[:, h, 2:3], scalar2=NEG_BIG,
                        op0=ALU.is_lt, op1=ALU.mult), [rg, r_mx, rpn], [rpn])
            tb, tb2 = 4 + b, 6 + b
            tpsA = self.psum[tb][0:16, :].bitcast(BF16).rearrange("p (h q) -> p h q", q=128)
            tpsB = self.psum[tb2][0:16, :].bitcast(BF16).rearrange("p (h q) -> p h q", q=128)

            def f_t(e, pn=pn, tpsA=tpsA, tpsB=tpsB):
                ins = None
                for h in range(16):
                    o = tpsA[:, h, :] if h < 8 else tpsB[:, h - 8, :]
                    ins = e.transpose(out=o, in_=pn[:, h, :], identity=self.identb[:])
                return ins
            self.op("pe", f_t, [rpn, self.r_const], [self.pres[tb], self.pres[tb2]])
            qt = qi // 4
            st, rst = pst[qt % 2], r_pst[qt % 2]
            sub = qi % 4
            self.op("act", lambda e, st=st, tpsA=tpsA, sub=sub: e.copy(out=st[:, 0:8, sub * 128:(sub + 1) * 128], in_=tpsA),
                    [self.pres[tb]], [rst])
            self.op("act", lambda e, st=st, tpsB=tpsB, sub=sub: e.copy(out=st[:, 8:16, sub * 128:(sub + 1) * 128], in_=tpsB),
                    [self.pres[tb2]], [rst])
            if sub == 3:
                for r in range(2):
                    self.dma("sp", ptv[:, r, :, qt * 512:(qt + 1) * 512], st[:, 8 * r:8 * r + 8, :], [rst], [self.r_pt])
        self.s.barrier()
        self.release(m0)

    def oproj_phase(self, wo):
        S = self.S
        T = 512
        NT = S // T
        m0 = self.mark()
        Wo, r_wo = wo
        OTs = [self.sb([128, NKC, T], BF16, "OTs") for _ in range(2)]
        r_ots = [Res("ots0"), Res("ots1")]
        hres = [self.sb([128, D], F32, "hres") for _ in range(2)]
        r_hres = [Res("hres0"), Res("hres1")]
        otv = self.OT.rearrange("(k p) s -> p k s", p=128)
        yc = 0
        hc = 0

        def load(t):
            self.dma("sp", OTs[t % 2][:], otv[:, :, t * T:(t + 1) * T], [self.r_ot], [r_ots[t % 2]])
        load(0)
        for t in range(NT):
            if t + 1 < NT:
                load(t + 1)
            ot, rot = OTs[t % 2], r_ots[t % 2]
            for ts in range(4):
                hb, rhb = hres[hc % 2], r_hres[hc % 2]
                hc += 1
                r0 = t * T + ts * 128
                self.dma("sp", hb[:], self.h[r0:r0 + 128, :], [self.r_h], [rhb])
                for half in range(2):
                    bk = yc % 4
                    yc += 1
                    yps = self.psum[bk][:, :]

                    def f_mm(e, ts=ts, half=half, yps=yps, ot=ot):
                        ins = None
                        for kc in range(NKC):
                            ins = e.matmul(yps, lhsT=ot[:, kc, ts * 128:(ts + 1) * 128],
                                           rhs=Wo[:, kc, half * 512:(half + 1) * 512],
                                           start=(kc == 0), stop=(kc == NKC - 1))
                        return ins
                    self.op("pe", f_mm, [rot, r_wo], [self.pres[bk]])
                    self.op("dve", lambda e, hb=hb, half=half, yps=yps: e.tensor_tensor(
                        out=hb[:, half * 512:(half + 1) * 512], in0=yps, in1=hb[:, half * 512:(half + 1) * 512],
                        op=ALU.add), [self.pres[bk], rhb], [rhb])
                self.dma("sp", self.h[r0:r0 + 128, :], hb[:], [rhb], [self.r_h])
        self.s.barrier()
        self.release(m0)

    def final_norm_phase(self):
        S = self.S
        m0 = self.mark()
        gb = self.sb([128, D], F32, "gb")
        r_gb = Res("gb")
        self.load_gain(gb, r_gb, self.final_g[0:1, :])
        hb = [self.sb([128, D], F32, "fh") for _ in range(2)]
        jk = self.sb([128, D], BF16, "fj")
        fs = [self.sb([128, 1], F32, "fs") for _ in range(2)]
        r_hb = [Res("fh0"), Res("fh1")]
        for t in range(S // 128):
            b, rb = hb[t % 2], r_hb[t % 2]
            f = fs[t % 2]
            self.dma("sp", b[:], self.h[t * 128:(t + 1) * 128, :], [self.r_h], [rb])
            self.op("act", lambda e, b=b, f=f: e.activation(out=jk[:], in_=b[:], func=AF.Square, accum_out=f[:]), [rb], [rb])

            self.op("pool", lambda e, f=f: e.tensor_scalar(out=f[:], in0=f[:], scalar1=1.0 / D, scalar2=RMS_EPS, op0=ALU.mult,
                                                           op1=ALU.add), [rb], [rb])
            self.op("pool", lambda e, f=f: e.tensor_tensor(out=f[:], in0=f[:], in1=self.neghalf[:], op=ALU.pow),
                    [rb, self.r_const], [rb])
            self.op("dve", lambda e, b=b, f=f: e.scalar_tensor_tensor(out=b[:], in0=b[:], scalar=f[:], in1=gb[:],
                                                                      op0=ALU.mult, op1=ALU.mult), [rb, r_gb], [rb])
            self.dma("sp", self.out[t * 128:(t + 1) * 128, :], b[:], [rb], [self.r_h])
        self.s.barrier()
        self.release(m0)

    def build(self):
        self.declare_io()
        self.setup_consts()
        sa = self.stop_after
        self.ffn_phase(self.x, 0, 0, final=(sa == "ffn0"))
        if sa == "ffn0":
            return self.finish()
        self.setup_attn_consts()
        self.qkv_phase(0, "diff")
        if sa == "qkv0":
            return self.finish()
        base = self.mark()
        wo = self.wo_load(0)
        self.attn_phase("diff")
        if sa == "attn0":
            return self.finish()
        if sa == "att0":
            self.oproj_phase(wo)
            self.final_norm_phase()
            return self.finish()
        self.ffn_phase(self.h, 0, 1, oproj=wo)
        self.release(base)
        self.ffn_phase(self.h, 1, 0)
        self.qkv_phase(1, "moba")
        self.gate_phase()
        if sa == "gate1":
            return self.finish()
        base = self.mark()
        wo = self.wo_load(1)
        self.attn_phase("moba")
        if sa == "attn1":
            return self.finish()
        if sa == "att1":
            self.oproj_phase(wo)
            self.final_norm_phase()
            return self.finish()
        self.ffn_phase(self.h, 1, 1, final=True, oproj=wo)
        self.release(base)
        return self.finish()

    def finish(self):
        self.s.barrier()
        self.s.emit(self.nc)
        return self.nc


def host_consts(S):
    ident = np.eye(128, dtype=np.float32)
    anti = np.ascontiguousarray(ident[::-1])
    d = np.arange(256)
    n = np.maximum(d, 0)
    nf = np.maximum(n, 1).astype(np.float32)
    large = 16 + (np.log(nf / np.float32(16)) / np.float32(np.log(128 / 16)) * np.float32(16)).astype(np.int32)
    large = np.minimum(large, 31)
    bucket = np.where(n < 16, n, large)
    oh = np.zeros((32, 256), np.float32)
    oh[bucket, d] = 1.0
    ind = np.zeros((16, S), np.float32)
    for b in range(S // 256):
        ind[b, b * 256:(b + 1) * 256] = 1.0
    return {"c_ident": ident, "c_anti": anti, "c_oh": oh, "c_ind": ind}


_CACHE = {}


def run(inputs, S, ncores, stop_after=None, trace=False, debug=False):
    key = (S, stop_after, debug)
    if key not in _CACHE:
        _CACHE[key] = Builder(S, stop_after, debug).build()
    nc = _CACHE[key]
    consts = host_consts(S)
    f = lambda a: np.ascontiguousarray(a, dtype=np.float32)
    shared = {
        "rel_bias": f(inputs["rel_bias"]),
        "norm_g": f(inputs["norm_g"]),
        "final_norm_g": f(inputs["final_norm_g"]).reshape(1, D),
        "ffn_w_in": f(inputs["ffn_w_in"]),
        "ffn_w_out": f(inputs["ffn_w_out"]),
        "diff_w_qkv": f(inputs["diff_w_qkv"][0]),
        "diff_lambda": f(inputs["diff_lambda"][0]).reshape(1, 256),
        "diff_subln_g": f(inputs["diff_subln_g"][0]).reshape(1, 128),
        "diff_w_o": f(inputs["diff_w_o"][0]),
        "moba_w_qkv": f(inputs["moba_w_qkv"][0]),
        "moba_w_o": f(inputs["moba_w_o"][0]),
    }
    shared.update(consts)
    x = f(inputs["x"])
    in_maps = []
    for c in range(ncores):
        m = dict(shared)
        m["x"] = x[c]
        in_maps.append(m)
    res = run_bass_kernel_spmd(nc, in_maps, core_ids=list(range(ncores)), trace=trace)
    out = np.stack([np.asarray(r["out"]) for r in res.results], axis=0)
    return out, res


def kernel(**inputs):
    out, _ = run(inputs, SEQ, NCORES)
    return out.astype(np.float32)
```

```python
import numpy as np
import concourse.bass as bass
import concourse.mybir as mybir
from concourse.bass_utils import run_bass_kernel_spmd

F32 = mybir.dt.float32
BF16 = mybir.dt.bfloat16
AF = mybir.ActivationFunctionType
ALU = mybir.AluOpType
AX = mybir.AxisListType

D = 1024
FF = 2816
NKC = 8
NFC = 22
SEQ = 4096
NCORES = 8
RMS_EPS = 1e-6
SUBLN_EPS = 1e-5
NEG_BIG = -30000.0


class Res:
    __slots__ = ("name", "w", "rs")

    def __init__(self, name=""):
        self.name = name
        self.w = None
        self.rs = []


class Op:
    __slots__ = ("eng", "fn", "deps", "ms", "need", "dma", "sem_idx", "semval")

    def __init__(self, eng, fn, dma):
        self.eng = eng
        self.fn = fn
        self.deps = ()
        self.ms = 0
        self.need = False
        self.dma = dma
        self.sem_idx = -1
        self.semval = 0


ENGS = ("pe", "act", "dve", "pool", "sp")


class Sched:
    def __init__(self, n_hw_sems=40, n_sw_sems=24):
        self.ops = {e: [] for e in ENGS}
        self.last_real = {}
        self.dmas_since_barrier = []
        self.dma_ops = {"hw": [], "sw": []}
        self.NSK = {"hw": n_hw_sems, "sw": n_sw_sems}
        self.SBASE = {"hw": 0, "sw": n_hw_sems}
        self.NS = n_hw_sems + n_sw_sems

    def add(self, eng, fn, reads=(), writes=(), dma=False):
        op = Op(eng, fn, dma)
        raw = set()
        other = set()
        for r in reads:
            if r.w is not None:
                raw.add(r.w)
        for w in writes:
            if w.w is not None:
                other.add(w.w)
            for rd in w.rs:
                other.add(rd)
        deps = set()
        for dpo in raw | other:
            if dpo is op:
                continue
            if dpo.dma or dma:
                deps.add(dpo)
            elif dpo.eng == eng:
                if eng != "pe" and dpo in raw:
                    deps.add(dpo)
            else:
                deps.add(dpo)
        if dma:
            kind = "sw" if eng == "pool" else "hw"
            lst, ns = self.dma_ops[kind], self.NSK[kind]
            k = len(lst)
            op.sem_idx = self.SBASE[kind] + k % ns
            op.semval = 16 * (k // ns + 1)
            if k >= ns:
                deps.add(lst[k - ns])
            lst.append(op)
            self.dmas_since_barrier.append(op)
        for dpo in deps:
            dpo.need = True
        op.deps = tuple(deps)
        for r in reads:
            r.rs.append(op)
        for w in writes:
            w.w = op
            w.rs = []
        self.ops[eng].append(op)
        if fn is not None and not dma:
            self.last_real[eng] = op
        return op

    def barrier(self):
        lasts = list(self.last_real.values()) + list(self.dmas_since_barrier)
        for e in ENGS:
            op = Op(e, None, False)
            deps = []
            for dpo in lasts:
                if (not dpo.dma) and dpo.eng == e:
                    continue
                dpo.need = True
                deps.append(dpo)
            op.deps = tuple(deps)
            self.ops[e].append(op)
        self.dmas_since_barrier = []

    def emit(self, nc):
        for e in ENGS:
            c = 0
            for op in self.ops[e]:
                if op.dma or op.fn is None:
                    continue
                if op.need:
                    c += 1
                op.ms = c
        import contextlib
        with contextlib.ExitStack() as st:
            esem = {e: st.enter_context(nc.semaphore("s_" + e)) for e in ("pe", "act", "dve", "pool")}
            dsem = [st.enter_context(nc.semaphore("d%d" % i)) for i in range(self.NS)]
            block = st.enter_context(nc.Block())

            def run(ename):
                def body(eng):
                    seen = {}
                    for op in self.ops[ename]:
                        for dpo in op.deps:
                            if dpo.dma:
                                key = ("d", dpo.sem_idx)
                                val = dpo.semval
                                sem = dsem[dpo.sem_idx]
                            else:
                                key = dpo.eng
                                val = dpo.ms
                                sem = esem[dpo.eng]
                            if seen.get(key, 0) >= val:
                                continue
                            eng.wait_ge(sem, val)
                            seen[key] = val
                        if op.fn is None:
                            continue
                        ins = op.fn(eng)
                        if op.dma:
                            ins.then_inc(dsem[op.sem_idx], 16)
                        elif op.need:
                            ins.then_inc(esem[ename], 1)
                return body

            block.tensor(run("pe"))
            block.scalar(run("act"))
            block.vector(run("dve"))
            block.gpsimd(run("pool"))
            block.sync(run("sp"))


class Builder:
    def __init__(self, S, stop_after=None, debug=False):
        self.S = S
        self.stop_after = stop_after
        self.debug = debug
        nc = bass.Bass("TRN2", target_bir_lowering=False)
        self.nc = nc
        self.s = Sched()
        lo, hi = nc.bump_sbuf(212000)
        self.sb_lo, self.sb_hi = lo, hi
        self.sb_off = lo
        self.cnt = 0
        self.psum = [nc.alloc_psum_tensor("bank%d" % i, [128, 512], F32) for i in range(8)]
        self.pres = [Res("bank%d" % i) for i in range(8)]

    def sb(self, shape, dtype, name="t"):
        esz = 4 if dtype == F32 else 2
        n = 1
        for d in shape[1:]:
            n *= d
        nbytes = (n * esz + 31) // 32 * 32
        off = self.sb_off
        assert off + nbytes <= self.sb_hi, "SBUF overflow %s %d" % (name, off + nbytes - self.sb_hi)
        self.sb_off += nbytes
        self.cnt += 1
        return self.nc.alloc_sbuf_tensor_at("%s_%d" % (name, self.cnt), list(shape), dtype, offset=off)

    def mark(self):
        return self.sb_off

    def release(self, m):
        self.sb_off = m

    def dram_in(self, name, shape, dtype=F32):
        return self.nc.dram_tensor(name, list(shape), dtype, kind="ExternalInput").ap()

    def dram_tmp(self, name, shape, dtype):
        kind = "ExternalOutput" if self.debug else "Internal"
        return self.nc.dram_tensor(name, list(shape), dtype, kind=kind).ap()

    def dma(self, q, out, in_, reads, writes, **kw):
        return self.s.add(q, lambda e, out=out, in_=in_, kw=kw: e.dma_start(out=out, in_=in_, **kw),
                          reads, writes, dma=True)

    def op(self, eng, fn, reads=(), writes=()):
        return self.s.add(eng, fn, reads, writes)

    def conv_schedule(self, key, src, dst):
        R = src.shape[0]
        res = Res("wb_" + key)
        self.r_wb[key] = res
        for r0 in range(0, R, 128):
            self.conv_queue.append((src[r0:r0 + 128, :], dst[r0:r0 + 128, :], res))

    def conv_pump(self, n=None):
        n = len(self.conv_queue) if n is None else n
        for _ in range(min(n, len(self.conv_queue))):
            s_, d_, res = self.conv_queue.pop(0)
            self.dma("pool", d_, s_, [], [res], max_dma_last_dim=8192)

    def declare_io(self):
        S = self.S
        self.x = self.dram_in("x", [S, D])
        self.rel_bias = self.dram_in("rel_bias", [32, 16])
        self.norm_g = self.dram_in("norm_g", [2, 3, D])
        self.final_g = self.dram_in("final_norm_g", [1, D])
        self.ffn_w_in = self.dram_in("ffn_w_in", [2, 2, D, 2 * FF])
        self.ffn_w_out = self.dram_in("ffn_w_out", [2, 2, FF, D])
        self.diff_w_qkv = self.dram_in("diff_w_qkv", [D, 3 * D])
        self.diff_lambda = self.dram_in("diff_lambda", [1, 256])
        self.diff_subln_g = self.dram_in("diff_subln_g", [1, 128])
        self.diff_w_o = self.dram_in("diff_w_o", [D, D])
        self.moba_w_qkv = self.dram_in("moba_w_qkv", [D, 3 * D])
        self.moba_w_o = self.dram_in("moba_w_o", [D, D])
        self.c_ident = self.dram_in("c_ident", [128, 128])
        self.c_anti = self.dram_in("c_anti", [128, 128])
        self.c_oh = self.dram_in("c_oh", [32, 256])
        self.c_ind = self.dram_in("c_ind", [16, S])
        self.out = self.nc.dram_tensor("out", [S, D], F32, kind="ExternalOutput").ap()
        self.h = self.dram_tmp("h_res", [S, D], F32)
        self.QT = self.dram_tmp("QT", [D, S], BF16)
        self.KT = self.dram_tmp("KT", [D, S], BF16)
        self.VA = self.dram_tmp("VA", [S, 1040], BF16)
        self.OT = self.dram_tmp("OT", [D, S], BF16)
        self.PT = self.dram_tmp("PT", [16, 16, S], BF16)
        self.EGd = self.dram_tmp("EGd", [16, 384], F32)
        self.DEd = self.dram_tmp("DEd", [2, 128, 16, 128], BF16)
        self.wb_in = [self.dram_tmp("wb_in%d" % i, [D, 2 * FF], BF16) for i in range(2)]
        self.wb_out = [self.dram_tmp("wb_out%d" % i, [FF, D], BF16) for i in range(2)]
        self.wb_o = [self.dram_tmp("wb_o%d" % i, [D, D], BF16) for i in range(2)]
        self.wb_qkv = self.dram_tmp("wb_qkv", [D, 3 * D], BF16)
        self.r_wb = {}
        self.conv_queue = []
        self.r_h = Res("h")
        self.r_qkv = Res("qkv")
        self.r_ot = Res("ot")
        self.r_pt = Res("pt")

    def setup_consts(self):
        nc = self.nc
        self.identb = self.sb([128, 128], BF16, "identb")
        self.r_const = Res("const")
        idf = self.sb([128, 128], F32, "identf")
        r_idf = Res("idf")
        self.dma("sp", idf[:], self.c_ident, [], [r_idf])
        self.op("dve", lambda e: e.tensor_copy(out=self.identb[:], in_=idf[:]), [r_idf], [self.r_const])
        self.neghalf = self.sb([128, 1], F32, "neghalf")
        self.op("pool", lambda e: e.memset(self.neghalf[:], -0.5), [], [self.r_const])
        self.onescol = self.sb([128, 1], F32, "ones")
        self.op("pool", lambda e: e.memset(self.onescol[:], 1.0), [], [self.r_const])
        self.epscol = self.sb([128, 1], F32, "epscol")
        self.op("pool", lambda e: e.memset(self.epscol[:], SUBLN_EPS), [], [self.r_const])

    def norm_bufs(self, n=2):
        nb = {"n": n, "i": 0}
        nb["h"] = [self.sb([128, D], F32, "nh") for _ in range(n)]
        nb["xn"] = [self.sb([128, D], BF16, "nxn") for _ in range(n)]
        nb["ss"] = [self.sb([128, 1], F32, "nss") for _ in range(n)]
        nb["ms"] = [self.sb([128, 1], F32, "nms") for _ in range(n)]
        nb["rstd"] = [self.sb([128, 1], F32, "nrs") for _ in range(n)]
        nb["rh"] = [Res("nh%d" % i) for i in range(n)]
        nb["rxn"] = [Res("nxn%d" % i) for i in range(n)]
        nb["rss"] = [Res("nss%d" % i) for i in range(n)]
        nb["rrs"] = [Res("nrs%d" % i) for i in range(n)]
        return nb

    def norm_pre(self, nb, src_rows, src_res, gb, r_gb, eps=RMS_EPS):
        i = nb["i"] % nb["n"]
        nb["i"] += 1
        hb, xn, ss, ms, rstd = nb["h"][i], nb["xn"][i], nb["ss"][i], nb["ms"][i], nb["rstd"][i]
        rh, rxn, rss, rrs = nb["rh"][i], nb["rxn"][i], nb["rss"][i], nb["rrs"][i]
        self.dma("sp", hb[:], src_rows, src_res, [rh])
        self.op("act", lambda e: e.activation(out=xn[:], in_=hb[:], func=AF.Square, accum_out=ss[:]),
                [rh], [rxn, rss])

        self.op("pool", lambda e: e.tensor_scalar(out=ms[:], in0=ss[:], scalar1=1.0 / D, scalar2=eps, op0=ALU.mult,
                                                  op1=ALU.add), [rss], [rrs])
        self.op("pool", lambda e: e.tensor_tensor(out=rstd[:], in0=ms[:], in1=self.neghalf[:], op=ALU.pow),
                [rrs, self.r_const], [rrs])
        self.op("dve", lambda e: e.scalar_tensor_tensor(out=xn[:], in0=hb[:], scalar=rstd[:], in1=gb[:],
                                                        op0=ALU.mult, op1=ALU.mult),
                [rh, rrs, r_gb], [rxn])
        return i

    def norm_T(self, nb, i, trbank, dst, r_dst, copy_eng="act"):
        xn, rxn = nb["xn"][i], nb["rxn"][i]
        ps = self.psum[trbank][:].bitcast(BF16).rearrange("p (k t) -> p k t", k=8)
        pr = self.pres[trbank]

        def f_tr(e):
            ins = None
            for kc in range(NKC):
                ins = e.transpose(out=ps[:, kc, :], in_=xn[:, kc * 128:(kc + 1) * 128], identity=self.identb[:])
            return ins
        self.op("pe", f_tr, [rxn, self.r_const], [pr])
        if copy_eng == "act":
            self.op("act", lambda e: e.copy(out=dst, in_=ps), [pr], [r_dst])
        else:
            self.op("dve", lambda e: e.tensor_copy(out=dst, in_=ps), [pr], [r_dst])

    def load_gain(self, gb, r_gb, src_row):
        self.dma("sp", gb[:], src_row.partition_broadcast(128), [], [r_gb])

    def ffn_load(self, l, which):
        w_in = self.ffn_w_in[l, which]
        w_out = self.ffn_w_out[l, which]
        Win = self.sb([128, NKC, 2 * FF], BF16, "Win")
        Wout = self.sb([128, NFC, D], BF16, "Wout")
        r_win = [Res("win%d" % k) for k in range(NKC)]
        r_wout = Res("wout")
        for kc in range(NKC):
            self.dma("pool", Win[:, kc, :], w_in[kc * 128:(kc + 1) * 128, :], [], [r_win[kc]],
                     max_dma_last_dim=8192)
        w_out_v = w_out.rearrange("(k p) d -> p k d", p=128)
        for k0 in range(0, NFC, 6):
            k1 = min(NFC, k0 + 6)
            self.dma("pool", Wout[:, k0:k1, :], w_out_v[:, k0:k1, :], [], [r_wout])
        return Win, Wout, r_win, r_wout

    def wo_load(self, li):
        w = self.diff_w_o if li == 0 else self.moba_w_o
        Wo = self.sb([128, NKC, D], BF16, "Wo")
        r_wo = Res("wo")
        wv = w.rearrange("(k p) d -> p k d", p=128)
        for k0 in range(0, NKC, 4):
            self.dma("pool", Wo[:, k0:k0 + 4, :], wv[:, k0:k0 + 4, :], [], [r_wo])
        return Wo, r_wo

    def ffn_phase(self, h_in, l, which, final=False, w=None, oproj=None):
        S = self.S
        T = 256
        NT = S // T
        m0 = self.mark()
        gidx = 0 if which == 0 else 2
        if w is None:
            Win, Wout, r_win, r_wout = self.ffn_load(l, which)
        else:
            Win, Wout, r_win, r_wout = w
        gb = self.sb([128, D], F32, "gb")
        r_gb = Res("gb")
        self.load_gain(gb, r_gb, self.norm_g[l, gidx:gidx + 1, :])
        if final:
            gfb = self.sb([128, D], F32, "gfb")
            r_gfb = Res("gfb")
            self.load_gain(gfb, r_gfb, self.final_g[0:1, :])
        hb4 = [self.sb([128, D], F32, "hb") for _ in range(4)]
        r_hb4 = [Res("hb%d" % i) for i in range(4)]
        xn2 = [self.sb([128, D], BF16, "xn") for _ in range(2)]
        r_xn2 = [Res("xn%d" % i) for i in range(2)]
        st = [[self.sb([128, 1], F32, "st") for _ in range(3)] for _ in range(2)]
        r_st = [Res("st%d" % i) for i in range(2)]
        xnT = [self.sb([128, NKC, T], BF16, "xnT") for _ in range(2)]
        r_xnT = [Res("xnT%d" % i) for i in range(2)]
        aT = self.sb([128, NFC, T], BF16, "aT")
        r_aT = [Res("aT%d" % j) for j in range(NFC)]
        sg2 = self.sb([128, 2, T], F32, "sg")
        sg = [sg2[:, 0, :], sg2[:, 1, :]]
        r_sg = [Res("sg%d" % i) for i in range(2)]
        if oproj is not None:
            Wo, r_wo = oproj
            OTs = [self.sb([128, NKC, T], BF16, "OTs") for _ in range(2)]
            r_ots = [Res("ots0"), Res("ots1")]
            otv = self.OT.rearrange("(k p) s -> p k s", p=128)
        if final:
            fss = [self.sb([128, 1], F32, "fss") for _ in range(2)]
            fjunk = sg2[:].rearrange("p a b -> p (a b)").bitcast(BF16)
        GB, UB, YB, TRB = (0, 1), (2, 3), (4, 5), (6, 7)
        yc = 0

        def hslot(t, ts):
            return (2 * t + ts) % 4

        def loads(t):
            for ts in range(2):
                k = hslot(t, ts)
                r0 = t * T + ts * 128
                self.dma("sp", hb4[k][:], h_in[r0:r0 + 128, :], [self.r_h], [r_hb4[k]])
            if oproj is not None:
                self.dma("sp", OTs[t % 2][:], otv[:, :, t * T:(t + 1) * T], [self.r_ot], [r_ots[t % 2]])

        def oproj_tile(t):
            nonlocal yc
            if oproj is None:
                return
            ot, rot = OTs[t % 2], r_ots[t % 2]
            for ts in range(2):
                k = hslot(t, ts)
                hb, rhb = hb4[k], r_hb4[k]
                for half in range(2):
                    ybk = YB[yc % 2]
                    yc += 1
                    yps = self.psum[ybk][:, :]

                    def f_mm(e, ts=ts, half=half, yps=yps, ot=ot):
                        ins = None
                        for kc in range(NKC):
                            ins = e.matmul(yps, lhsT=ot[:, kc, ts * 128:(ts + 1) * 128],
                                           rhs=Wo[:, kc, half * 512:(half + 1) * 512],
                                           start=(kc == 0), stop=(kc == NKC - 1))
                        return ins
                    self.op("pe", f_mm, [rot, r_wo], [self.pres[ybk]])
                    self.op("dve", lambda e, hb=hb, half=half, yps=yps: e.tensor_tensor(
                        out=hb[:, half * 512:(half + 1) * 512], in0=yps, in1=hb[:, half * 512:(half + 1) * 512],
                        op=ALU.add), [self.pres[ybk], rhb], [rhb])

        def norm_chain(t):
            for ts in range(2):
                k = hslot(t, ts)
                hb, rhb = hb4[k], r_hb4[k]
                xn, rxn = xn2[ts], r_xn2[ts]
                ss, ms, rstd = st[ts]
                rst = r_st[ts]
                self.op("act", lambda e, xn=xn, hb=hb, ss=ss: e.activation(out=xn[:], in_=hb[:], func=AF.Square,
                                                                          accum_out=ss[:]), [rhb], [rxn, rst])

                self.op("pool", lambda e, ss=ss, ms=ms: e.tensor_scalar(out=ms[:], in0=ss[:], scalar1=1.0 / D, scalar2=RMS_EPS,
                                                                        op0=ALU.mult, op1=ALU.add), [rst], [rst])
                self.op("pool", lambda e, ms=ms, rstd=rstd: e.tensor_tensor(out=rstd[:], in0=ms[:], in1=self.neghalf[:],
                                                                            op=ALU.pow), [rst, self.r_const], [rst])
                self.op("dve", lambda e, xn=xn, hb=hb, rstd=rstd: e.scalar_tensor_tensor(
                    out=xn[:], in0=hb[:], scalar=rstd[:], in1=gb[:], op0=ALU.mult, op1=ALU.mult),
                    [rhb, rst, r_gb], [rxn])

        def norm_T_tile(t):
            for ts in range(2):
                xn, rxn = xn2[ts], r_xn2[ts]
                ps = self.psum[TRB[ts]][:].bitcast(BF16).rearrange("p (k t) -> p k t", k=8)
                pr = self.pres[TRB[ts]]

                def f_tr(e, xn=xn, ps=ps):
                    ins = None
                    for kc in range(NKC):
                        ins = e.transpose(out=ps[:, kc, :], in_=xn[:, kc * 128:(kc + 1) * 128], identity=self.identb[:])
                    return ins
                self.op("pe", f_tr, [rxn, self.r_const], [pr])
                dst = xnT[t % 2][:, :, ts * 128:(ts + 1) * 128]
                self.op("act", lambda e, dst=dst, ps=ps: e.copy(out=dst, in_=ps), [pr], [r_xnT[t % 2]])

        loads(0)
        oproj_tile(0)
        norm_chain(0)
        norm_T_tile(0)
        pair = 0
        for t in range(NT):
            xt = xnT[t % 2]
            rxt = r_xnT[t % 2]
            if t + 1 < NT:
                loads(t + 1)
            for j in range(NFC):
                gbk, ubk = GB[pair % 2], UB[pair % 2]
                sgt, rsg = sg[pair % 2], r_sg[pair % 2]
                pair += 1
                gps = self.psum[gbk][:, 0:T]
                ups = self.psum[ubk][:, 0:T]

                def f_mm1(e, j=j, gps=gps, ups=ups, xt=xt):
                    for kc in range(NKC):
                        e.matmul(gps, lhsT=Win[:, kc, j * 128:(j + 1) * 128], rhs=xt[:, kc, :],
                                 start=(kc == 0), stop=(kc == NKC - 1))
                    ins = None
                    for kc in range(NKC):
                        ins = e.matmul(ups, lhsT=Win[:, kc, FF + j * 128:FF + (j + 1) * 128], rhs=xt[:, kc, :],
                                       start=(kc == 0), stop=(kc == NKC - 1))
                    return ins
                self.op("pe", f_mm1, [rxt] + r_win, [self.pres[gbk], self.pres[ubk]])
                self.op("act", lambda e, sgt=sgt, gps=gps: e.activation(out=sgt, in_=gps, func=AF.Silu),
                        [self.pres[gbk]], [rsg])
                self.op("dve", lambda e, j=j, sgt=sgt, ups=ups: e.tensor_tensor(out=aT[:, j, :], in0=sgt, in1=ups,
                                                                                 op=ALU.mult),
                        [rsg, self.pres[ubk]], [r_aT[j]])
                if j == 3 and t + 1 < NT:
                    oproj_tile(t + 1)
                    norm_chain(t + 1)
            if t + 1 < NT:
                norm_T_tile(t + 1)
            for ts in range(2):
                k = hslot(t, ts)
                hb, rhb = hb4[k], r_hb4[k]
                fs = fss[ts] if final else None
                r0 = t * T + ts * 128
                for half in range(2):
                    ybk = YB[yc % 2]
                    yc += 1
                    yps = self.psum[ybk][:, :]

                    def f_mm2(e, ts=ts, half=half, yps=yps):
                        ins = None
                        for kc in range(NFC):
                            ins = e.matmul(yps, lhsT=aT[:, kc, ts * 128:(ts + 1) * 128],
                                           rhs=Wout[:, kc, half * 512:(half + 1) * 512],
                                           start=(kc == 0), stop=(kc == NFC - 1))
                        return ins
                    self.op("pe", f_mm2, r_aT + [r_wout], [self.pres[ybk]])
                    self.op("dve", lambda e, hb=hb, half=half, yps=yps: e.scalar_tensor_tensor(
                        out=hb[:, half * 512:(half + 1) * 512], in0=yps, scalar=0.5,
                        in1=hb[:, half * 512:(half + 1) * 512], op0=ALU.mult, op1=ALU.add),
                        [self.pres[ybk], rhb], [rhb])
                if not final:
                    self.dma("sp", self.h[r0:r0 + 128, :], hb[:], [rhb], [self.r_h])
                else:
                    self.op("act", lambda e, hb=hb, fs=fs: e.activation(out=fjunk, in_=hb[:], func=AF.Square,
                                                                      accum_out=fs[:]), [rhb] + r_sg, [rhb] + r_sg)

                    self.op("pool", lambda e, fs=fs: e.tensor_scalar(out=fs[:], in0=fs[:], scalar1=1.0 / D, scalar2=RMS_EPS,
                                                                     op0=ALU.mult, op1=ALU.add), [rhb], [rhb])
                    self.op("pool", lambda e, fs=fs: e.tensor_tensor(out=fs[:], in0=fs[:], in1=self.neghalf[:], op=ALU.pow),
                            [rhb, self.r_const], [rhb])
                    self.op("dve", lambda e, hb=hb, fs=fs: e.scalar_tensor_tensor(
                        out=hb[:], in0=hb[:], scalar=fs[:], in1=gfb[:], op0=ALU.mult, op1=ALU.mult),
                        [rhb, r_gfb], [rhb])
                    self.dma("sp", self.out[r0:r0 + 128, :], hb[:], [rhb], [self.r_h])
        self.s.barrier()
        self.release(m0)

    def setup_attn_consts(self):
        self.neglam = self.sb([128, 1], F32, "neglam")
        self.gsb = self.sb([128, 128], F32, "gsb")
        m0 = self.mark()
        self.DE = [self.sb([128, 16, 128], BF16, "DE%d" % ty) for ty in range(2)]
        self.r_DE = Res("DE")
        self.r_DEd = Res("DEd")
        rb = self.sb([32, 16], F32, "rb")
        oh = self.sb([32, 256], F32, "oh")
        r_rb, r_oh = Res("rb"), Res("oh")
        self.dma("sp", rb[:], self.rel_bias, [], [r_rb])
        self.dma("sp", oh[:], self.c_oh, [], [r_oh])
        gps = self.psum[0][0:16, 0:256]
        self.op("pe", lambda e: e.matmul(gps, lhsT=rb[:], rhs=oh[:], start=True, stop=True), [r_rb, r_oh], [self.pres[0]])
        cv = self.sb([16, 1], F32, "cv")
        gs = self.sb([16, 256], F32, "gs")
        egf = self.sb([16, 384], F32, "egf")
        r_cv, r_gs, r_egf = Res("cv"), Res("gs"), Res("egf")
        self.op("dve", lambda e: e.tensor_copy(out=cv[:], in_=gps[:, 255:256]), [self.pres[0]], [r_cv])
        self.op("dve", lambda e: e.tensor_scalar(out=gs[:], in0=gps, scalar1=cv[:], scalar2=None, op0=ALU.subtract),
                [self.pres[0], r_cv], [r_gs])
        self.op("pool", lambda e: e.memset(egf[:, 0:128], 0.0), [], [r_egf])
        self.op("act", lambda e: e.activation(out=egf[:, 128:384], in_=gs[:], func=AF.Exp), [r_gs, r_egf], [r_egf])
        r_egd = Res("egd")
        self.dma("sp", self.EGd, egf[:], [r_egf], [r_egd])
        anti = self.sb([128, 128], F32, "anti")
        r_anti = Res("anti")
        self.dma("sp", anti[:], self.c_anti, [], [r_anti])
        hd = [self.sb([128, 16, 128], F32, "hd") for _ in range(2)]
        r_hd = [Res("hd0"), Res("hd1")]
        for ty in range(2):
            src = bass.AP(tensor=self.EGd.tensor, offset=1 + 128 * ty, ap=[[1, 128], [384, 16], [1, 128]])
            self.dma("sp", hd[ty][:], src, [r_egd], [r_hd[ty]])
        for ty in range(2):
            hv = hd[ty][:].rearrange("p m q -> p (m q)")
            dv = self.DE[ty][:].rearrange("p m q -> p (m q)")
            for c in range(4):
                bk = 1 + (ty * 4 + c) % 4
                ps = self.psum[bk][:, :]
                self.op("pe", lambda e, ps=ps, hv=hv, c=c: e.matmul(ps, lhsT=anti[:], rhs=hv[:, c * 512:(c + 1) * 512],
                                                                   start=True, stop=True),
                        [r_anti, r_hd[ty]], [self.pres[bk]])
                self.op("dve", lambda e, ps=ps, dv=dv, c=c: e.tensor_copy(out=dv[:, c * 512:(c + 1) * 512], in_=ps),
                        [self.pres[bk]], [self.r_DE])
        lp = self.sb([128, 256], F32, "lp")
        r_lp = Res("lp")
        self.dma("sp", lp[:], self.diff_lambda.partition_broadcast(128), [], [r_lp])
        lt = self.sb([128, 2, 64], F32, "lt")
        s12 = self.sb([128, 2], F32, "s12")
        e12 = self.sb([128, 2], F32, "e12")
        r_lt, r_s12, r_e12 = Res("lt"), Res("s12"), Res("e12")
        self.r_lam = Res("lam")
        lv = lp[:].rearrange("p (a b d) -> p a b d", a=2, b=2)
        self.op("dve", lambda e: e.tensor_tensor(out=lt[:], in0=lv[:, :, 0, :], in1=lv[:, :, 1, :], op=ALU.mult), [r_lp], [r_lt])
        self.op("dve", lambda e: e.tensor_reduce(out=s12[:], in_=lt[:], axis=AX.X, op=ALU.add), [r_lt], [r_s12])
        self.op("act", lambda e: e.activation(out=e12[:], in_=s12[:], func=AF.Exp), [r_s12], [r_e12])

        self.op("pool", lambda e: e.tensor_tensor(out=self.neglam[:], in0=e12[:, 1:2], in1=e12[:, 0:1], op=ALU.subtract),
                [r_e12], [self.r_lam])
        self.op("pool", lambda e: e.tensor_scalar(out=self.neglam[:], in0=self.neglam[:], scalar1=-0.2, scalar2=None,
                                                  op0=ALU.add), [self.r_lam], [self.r_lam])
        r_g0 = Res("g0")
        self.dma("sp", self.gsb[:], self.diff_subln_g.partition_broadcast(128), [], [r_g0])
        self.op("pool", lambda e: e.tensor_scalar(out=self.gsb[:], in0=self.gsb[:], scalar1=0.8, scalar2=None, op0=ALU.mult),
                [r_g0], [self.r_lam])
        for ty in range(2):
            self.dma("sp", self.DEd[ty], self.DE[ty][:], [self.r_DE], [self.r_DEd])
        self.s.barrier()
        self.release(m0)

    def qkv_phase(self, l, mode):
        S = self.S
        T = 512
        NT = S // T
        m0 = self.mark()
        w = self.diff_w_qkv if mode == "diff" else self.moba_w_qkv
        W = self.sb([128, NKC, 3 * D], BF16, "Wqkv")
        r_w = [Res("wqkv%d" % k) for k in range(NKC)]
        for kc in range(NKC):
            self.dma("pool", W[:, kc, :], w[kc * 128:(kc + 1) * 128, :], [], [r_w[kc]], max_dma_last_dim=8192)
        gb = self.sb([128, D], F32, "gb")
        r_gb = Res("gb")
        self.load_gain(gb, r_gb, self.norm_g[l, 1:2, :])
        nb = self.norm_bufs(4)
        hnT = [self.sb([128, NKC, T], BF16, "hnT") for _ in range(2)]
        r_hnT = [Res("hnT%d" % i) for i in range(2)]
        qst = [self.sb([128, T], BF16, "qst") for _ in range(4)]
        r_qst = [Res("qst%d" % i) for i in range(4)]
        vst = [self.sb([128, 1040], BF16, "vst") for _ in range(2)]
        r_vst = [Res("vst%d" % i) for i in range(2)]
        for i in range(2):
            self.op("pool", lambda e, i=i: e.memset(vst[i][:], 1.0), [], [r_vst[i]])
        QKB, VB, TRB = (0, 1, 2, 3), (4, 5), (6, 7)

        def pre(t):
            return [self.norm_pre(nb, self.h[t * T + ts * 128: t * T + (ts + 1) * 128, :], [self.r_h], gb, r_gb)
                    for ts in range(4)]

        def post(t, slots):
            for ts, i in enumerate(slots):
                self.norm_T(nb, i, TRB[ts % 2], hnT[t % 2][:, :, ts * 128:(ts + 1) * 128], r_hnT[t % 2],
                            copy_eng=("act" if ts % 2 == 0 else "dve"))
        slots = pre(0)
        post(0, slots)
        qc = 0
        vc = 0
        for t in range(NT):
            ht, rht = hnT[t % 2], r_hnT[t % 2]
            if t + 1 < NT:
                nslots = pre(t + 1)
            for c in range(16):
                bk = QKB[qc % 4]
                st, rst = qst[qc % 4], r_qst[qc % 4]
                ps = self.psum[bk][:, :]

                def f_mm(e, c=c, ps=ps, ht=ht):
                    ins = None
                    for kc in range(NKC):
                        ins = e.matmul(ps, lhsT=W[:, kc, c * 128:(c + 1) * 128], rhs=ht[:, kc, :],
                                       start=(kc == 0), stop=(kc == NKC - 1))
                    return ins
                self.op("pe", f_mm, [rht] + r_w, [self.pres[bk]])
                if qc % 2 == 0:
                    self.op("act", lambda e, st=st, ps=ps: e.copy(out=st[:], in_=ps), [self.pres[bk]], [rst])
                else:
                    self.op("dve", lambda e, st=st, ps=ps: e.tensor_copy(out=st[:], in_=ps), [self.pres[bk]], [rst])
                dst = self.QT if c < 8 else self.KT
                cc = c % 8
                self.dma("sp", dst[cc * 128:(cc + 1) * 128, t * T:(t + 1) * T], st[:], [rst], [self.r_qkv])
                qc += 1
            if t + 1 < NT:
                post(t + 1, nslots)
            for ts in range(4):
                vt, rvt = vst[vc % 2], r_vst[vc % 2]
                vc += 1
                for half in range(2):
                    bk = VB[half]
                    ps = self.psum[bk][:, :]

                    def f_mv(e, ts=ts, half=half, ps=ps, ht=ht):
                        ins = None
                        for kc in range(NKC):
                            ins = e.matmul(ps, lhsT=ht[:, kc, ts * 128:(ts + 1) * 128],
                                           rhs=W[:, kc, 2 * D + half * 512: 2 * D + (half + 1) * 512],
                                           start=(kc == 0), stop=(kc == NKC - 1))
                        return ins
                    self.op("pe", f_mv, [rht] + r_w, [self.pres[bk]])
                    if mode == "diff":
                        o = vt[:, 0:1032].rearrange("p (h c) -> p h c", c=129)[:, 4 * half:4 * half + 4, 0:128]
                        i_ = ps.rearrange("p (h c) -> p h c", c=128)
                    else:
                        o = vt[:, 0:1040].rearrange("p (h c) -> p h c", c=65)[:, 8 * half:8 * half + 8, 0:64]
                        i_ = ps.rearrange("p (h c) -> p h c", c=64)
                    if half == 0:
                        self.op("act", lambda e, o=o, i_=i_: e.copy(out=o, in_=i_), [self.pres[bk]], [rvt])
                    else:
                        self.op("dve", lambda e, o=o, i_=i_: e.tensor_copy(out=o, in_=i_), [self.pres[bk]], [rvt])
                r0 = t * T + ts * 128
                self.dma("sp", self.VA[r0:r0 + 128, :], vt[:], [rvt], [self.r_qkv])
        self.s.barrier()
        self.release(m0)

    def attn_phase(self, mode):
        S = self.S
        NQT = S // 512
        NKT = S // 128
        m0 = self.mark()
        diff = (mode == "diff")
        VAs = self.sb([128, NKT, 1040], BF16, "VAs")
        r_va = Res("va")
        vav = self.VA.rearrange("(k p) c -> p k c", p=128)
        step = max(1, NKT // 4)
        for k0 in range(0, NKT, step):
            self.dma("sp", VAs[:, k0:k0 + step, :], vav[:, k0:k0 + step, :], [self.r_qkv], [r_va])
        nheads = 8 if diff else 16
        nmaps = 2 if diff else 1
        DE = [self.sb([128, 16, 128], BF16, "DE%d" % ty) for ty in range(2)]
        r_DE = Res("DEl")
        for ty in range(2):
            self.dma("sp", DE[ty][:], self.DEd[ty], [self.r_DEd], [r_DE])
        Qs = [self.sb([128, S], BF16, "Qs") for _ in range(2)]
        r_q = [Res("q0"), Res("q1")]
        r_k = [Res("k0"), Res("k1")]
        r_cst = Res("acst")
        if diff:
            KA = [self.sb([128, S], BF16, "KA") for _ in range(2)]
            KB = [self.sb([128, S], BF16, "KB") for _ in range(2)]
            for b in range(2):
                self.op("pool", lambda e, b=b: e.memset(KA[b][64:128, :], 0.0), [], [r_k[b]])
                self.op("pool", lambda e, b=b: e.memset(KB[b][0:64, :], 0.0), [], [r_k[b]])
            ones_b = self.sb([128, 128], BF16, "ones_b")
            ones_f = self.sb([128, 128], F32, "ones_f")
            gcol = self.sb([128, 1], F32, "gcol")
            self.op("pool", lambda e: e.memset(ones_b[:], 1.0), [], [r_cst])
            self.op("pool", lambda e: e.memset(ones_f[:], 1.0), [], [r_cst])
            r_gc = Res("gc")
            self.dma("sp", gcol[:], self.diff_subln_g.rearrange("o v -> v o"), [], [r_gc])
            self.op("pool", lambda e: e.tensor_scalar(out=gcol[:], in0=gcol[:], scalar1=0.8, scalar2=None, op0=ALU.mult),
                    [r_gc], [r_cst])
        else:
            Ks = [self.sb([128, S], BF16, "Ks") for _ in range(2)]
            for b in range(2):
                self.op("pool", lambda e, b=b: e.memset(Qs[b][64:128, :], 0.0), [], [r_q[b]])
                self.op("pool", lambda e, b=b: e.memset(Ks[b][64:128, :], 0.0), [], [r_k[b]])
                self.dma("pool", Ks[b][64:80, :], self.c_ind, [], [r_k[b]], max_dma_last_dim=8192)
            sel_f = self.sb([65, 64], F32, "sel_f")
            self.op("pool", lambda e: e.memset(sel_f[:], 0.0), [], [r_cst])
            self.op("pool", lambda e: e.memset(sel_f[64:65, :], 1.0), [], [r_cst])
        NPT = 4 if diff else 6
        Pt = [self.sb([128, 512], BF16, "Pt") for _ in range(NPT)]
        r_pt = [Res("pt%d" % i) for i in range(NPT)]
        ost = [self.sb([128, 512], BF16, "ost") for _ in range(2)]
        r_ost = [Res("ost0"), Res("ost1")]
        if diff:
            f32t = lambda n: self.sb([128, 512], F32, n)
            e_s0, e_s1, e_t0, e_t1, e_o, e_sq, e_l, e_rs, e_c = [f32t("e%d" % i) for i in range(9)]
            r_e = Res("ep")
            r_e2 = Res("ep2")
            r_es = Res("es")
            r_ec = Res("ec")
        else:
            oa = [self.sb([65, 512], F32, "oa") for _ in range(2)]
            r_oa = [Res("oa0"), Res("oa1")]
            rcp = self.sb([64, 512], F32, "rcp")
            r_rcp = Res("rcp")
        SB = (0, 1, 2) if diff else (0, 1, 2, 6, 7)
        NSB = len(SB)
        sc = 0
        pc = 0
        LOOK = 2 if diff else 4

        def load_head(hh):
            b = hh % 2
            if diff:
                self.dma("sp", Qs[b][:], self.QT[hh * 128:(hh + 1) * 128, :], [self.r_qkv], [r_q[b]])
                self.dma("sp", KA[b][0:64, :], self.KT[hh * 128:hh * 128 + 64, :], [self.r_qkv], [r_k[b]])
                self.dma("sp", KB[b][64:128, :], self.KT[hh * 128 + 64:hh * 128 + 128, :], [self.r_qkv], [r_k[b]])
            else:
                self.dma("sp", Qs[b][0:64, :], self.QT[hh * 64:(hh + 1) * 64, :], [self.r_qkv], [r_q[b]])
                self.dma("sp", Qs[b][64:80, :], self.PT[hh], [self.r_pt], [r_q[b]])
                self.dma("sp", Ks[b][0:64, :], self.KT[hh * 64:(hh + 1) * 64, :], [self.r_qkv], [r_k[b]])

        load_head(0)
        deferred = []
        it = 0
        for hh in range(nheads):
            hb = hh % 2
            if hh + 1 < nheads:
                load_head(hh + 1)
            Qh = Qs[hb]
            for j in range(NQT):
                nkt = 4 * j + 4
                if diff:
                    obank = (3, 4)
                    sbank = (5, 6)
                    vcol, VM = hh * 129, 128
                    kmats = (KA[hb], KB[hb])
                else:
                    obank = (3 + it % 2,)
                    vcol, VM = hh * 65, 65
                    kmats = (Ks[hb],)
                units = [(i, kt) for kt in range(nkt) for i in range(nmaps)]
                pend = []

                def do_qk(i, kt, j=j, hh=hh, hb=hb, Qh=Qh, kmats=kmats, pend=pend):
                    nonlocal sc, pc
                    bk = SB[sc % NSB]
                    sc += 1
                    pt, rpt = Pt[pc % NPT], r_pt[pc % NPT]
                    pc += 1
                    m = (2 * hh + i) if diff else hh
                    s0 = max(0, kt - 4 * j)
                    c0 = s0 * 128
                    ps = self.psum[bk][:, c0:512]
                    lhs = kmats[i][:, kt * 128:(kt + 1) * 128]
                    rhs = Qh[:, j * 512 + c0:(j + 1) * 512]
                    self.op("pe", lambda e, ps=ps, lhs=lhs, rhs=rhs: e.matmul(ps, lhsT=lhs, rhs=rhs, start=True, stop=True),
                            [r_q[hb], r_k[hb]], [self.pres[bk]])
                    self.op("act", lambda e, ps=ps, pt=pt, c0=c0: e.activation(out=pt[:, c0:512], in_=ps, func=AF.Exp,
                                                                           scale=0.125),
                            [self.pres[bk]], [rpt])
                    for s in range(s0, 4):
                        qi = 4 * j + s
                        if kt == qi or kt == qi - 1:
                            ty = 0 if kt == qi else 1
                            de = DE[ty][:, m, :]
                            self.op("dve", lambda e, pt=pt, s=s, de=de: e.tensor_tensor(
                                out=pt[:, s * 128:(s + 1) * 128], in0=pt[:, s * 128:(s + 1) * 128],
                                in1=de, op=ALU.mult), [rpt, r_DE], [rpt])
                    pend.append((i, kt, pt, rpt, c0))

                def do_pv(j=j, obank=obank, vcol=vcol, VM=VM, pend=pend, nkt=nkt):
                    i, kt, pt, rpt, c0 = pend.pop(0)
                    ob = obank[i]
                    vst_ = VAs[:, kt, vcol:vcol + VM]
                    oacc = self.psum[ob][0:VM, c0:512]
                    if diff:
                        sb_ = sbank[i]
                        sacc = self.psum[sb_][:, c0:512]

                        def f_pv(e, kt=kt, pt=pt, c0=c0, oacc=oacc, sacc=sacc, vst_=vst_):
                            e.matmul(oacc, lhsT=vst_, rhs=pt[:, c0:512], start=(kt == 0), stop=(kt == nkt - 1))
                            return e.matmul(sacc, lhsT=ones_b[:], rhs=pt[:, c0:512], start=(kt == 0), stop=(kt == nkt - 1))
                        self.op("pe", f_pv, [rpt, r_va, r_cst], [self.pres[ob], self.pres[sb_]])
                    else:
                        self.op("pe", lambda e, kt=kt, pt=pt, c0=c0, oacc=oacc, vst_=vst_: e.matmul(
                            oacc, lhsT=vst_, rhs=pt[:, c0:512], start=(kt == 0), stop=(kt == nkt - 1)),
                            [rpt, r_va], [self.pres[ob]])

                nun = len(units)
                for idx in range(nun + LOOK):
                    if idx < nun:
                        do_qk(*units[idx])
                    if idx >= LOOK:
                        do_pv()
                    while deferred and deferred[0][0] <= idx:
                        deferred.pop(0)[1]()
                while deferred:
                    deferred.pop(0)[1]()
                k = it % 2
                if diff:
                    oa_, ob_, sa_, sb2_ = (self.psum[b][:, :] for b in (3, 4, 5, 6))
                    self.op("dve", lambda e, sa_=sa_: e.tensor_scalar(out=e_s0[:], in0=sa_, scalar1=2.0 ** -10, scalar2=None,
                                                                     op0=ALU.mult), [self.pres[5]], [r_es])
                    self.op("dve", lambda e, sb2_=sb2_: e.tensor_scalar(out=e_s1[:], in0=sb2_, scalar1=2.0 ** -10, scalar2=None,
                                                                       op0=ALU.mult), [self.pres[6]], [r_es])
                    self.op("dve", lambda e, oa_=oa_: e.tensor_tensor(out=e_t0[:], in0=oa_, in1=e_s1[:], op=ALU.mult),
                            [self.pres[3], r_es], [r_e])
                    self.op("dve", lambda e, ob_=ob_: e.tensor_tensor(out=e_t1[:], in0=ob_, in1=e_s0[:], op=ALU.mult),
                            [self.pres[4], r_es], [r_e])
                    self.op("dve", lambda e: e.scalar_tensor_tensor(out=e_o[:], in0=e_t1[:], scalar=self.neglam[:],
                                                                    in1=e_t0[:], op0=ALU.mult, op1=ALU.add),
                            [r_e, self.r_lam], [r_e])
                    self.op("pool", lambda e: e.tensor_tensor(out=e_c[:], in0=e_s0[:], in1=e_s1[:], op=ALU.mult), [r_es], [r_ec])
                    self.op("pool", lambda e: e.tensor_tensor(out=e_c[:], in0=e_c[:], in1=e_c[:], op=ALU.mult), [r_ec], [r_ec])
                    self.op("pool", lambda e: e.tensor_scalar(out=e_c[:], in0=e_c[:], scalar1=SUBLN_EPS, scalar2=1.0,
                                                              op0=ALU.mult, op1=ALU.mult), [r_ec], [r_ec])
                    self.op("dve", lambda e: e.tensor_tensor(out=e_sq[:], in0=e_o[:], in1=e_o[:], op=ALU.mult), [r_e], [r_e])

                    def st1():
                        self.op("pe", lambda e: e.matmul(self.psum[7][:, :], lhsT=ones_f[:], rhs=e_sq[:], start=True, stop=True),
                                [r_e, r_cst], [self.pres[7]])

                    def st2(hh=hh, j=j, k=k):
                        self.op("dve", lambda e: e.scalar_tensor_tensor(out=e_l[:], in0=self.psum[7][:, :], scalar=1.0 / 128,
                                                                        in1=e_c[:], op0=ALU.mult, op1=ALU.add),
                                [self.pres[7], r_ec], [r_e2])
                        self.op("act", lambda e: e.activation(out=e_l[:], in_=e_l[:], func=AF.Ln), [r_e2], [r_e2])
                        self.op("act", lambda e: e.activation(out=e_rs[:], in_=e_l[:], func=AF.Exp, scale=-0.5), [r_e2], [r_e2])
                        self.op("dve", lambda e, k=k: e.scalar_tensor_tensor(out=ost[k][:], in0=e_o[:], scalar=gcol[:],
                                                                              in1=e_rs[:], op0=ALU.mult, op1=ALU.mult),
                                [r_e, r_e2, r_cst], [r_ost[k], r_e])
                        self.dma("sp", self.OT[hh * 128:(hh + 1) * 128, j * 512:(j + 1) * 512], ost[k][:], [r_ost[k]],
                                 [self.r_ot])
                    deferred.append((5, st1))
                    deferred.append((10, st2))
                else:
                    ob = obank[0]
                    oak, roak = oa[k], r_oa[k]
                    sbk = 5
                    self.op("dve", lambda e, ob=ob, oak=oak: e.tensor_copy(out=oak[:], in_=self.psum[ob][0:65, :]),
                            [self.pres[ob]], [roak])

                    def st1(oak=oak, roak=roak, sbk=sbk):
                        self.op("pe", lambda e: e.matmul(self.psum[sbk][0:64, :], lhsT=sel_f[:], rhs=oak[:], start=True, stop=True),
                                [roak, r_cst], [self.pres[sbk]])

                    def st2(hh=hh, j=j, k=k, oak=oak, roak=roak, sbk=sbk):
                        self.op("dve", lambda e: e.reciprocal(out=rcp[:], in_=self.psum[sbk][0:64, :]), [self.pres[sbk]], [r_rcp])
                        self.op("dve", lambda e: e.tensor_tensor(out=ost[k][0:64, :], in0=oak[0:64, :], in1=rcp[:], op=ALU.mult),
                                [roak, r_rcp], [r_ost[k]])
                        self.dma("sp", self.OT[hh * 64:(hh + 1) * 64, j * 512:(j + 1) * 512], ost[k][0:64, :], [r_ost[k]],
                                 [self.r_ot])
                    deferred.append((4, st1))
                    deferred.append((8, st2))
                it += 1
        while deferred:
            deferred.pop(0)[1]()
        self.s.barrier()
        self.release(m0)

    def gate_phase(self):
        S = self.S
        NB = S // 256
        NQ = S // 128
        m0 = self.mark()
        QTa = self.sb([128, 8, S], BF16, "QTa")
        r_qta = Res("qta")
        qv = self.QT.rearrange("(c p) s -> p c s", p=128)
        for c0 in range(0, 8, 2):
            self.dma("sp", QTa[:, c0:c0 + 2, :], qv[:, c0:c0 + 2, :], [self.r_qkv], [r_qta])
        KTb = [self.sb([128, S], BF16, "KTb") for _ in range(2)]
        r_ktb = [Res("ktb0"), Res("ktb1")]
        KM = self.sb([128, 8, 16], F32, "KM")
        KMb = self.sb([128, 8, 16], BF16, "KMb")
        r_km, r_kmb = Res("km"), Res("kmb")
        self.op("pool", lambda e: e.memset(KM[:], 0.0), [], [r_km])
        for c in range(8):
            kb, rkb = KTb[c % 2], r_ktb[c % 2]
            self.dma("sp", kb[:], self.KT[c * 128:(c + 1) * 128, :], [self.r_qkv], [rkb])
            self.op("dve", lambda e, c=c, kb=kb: e.tensor_reduce(out=KM[:, c, 0:NB],
                                                                in_=kb[:].rearrange("p (n l) -> p n l", l=256),
                                                                axis=AX.X, op=ALU.add), [rkb, r_km], [r_km])
        self.op("dve", lambda e: e.tensor_copy(out=KMb[:], in_=KM[:]), [r_km], [r_kmb])
        gs = [self.sb([128, 16, 16], F32, "gs") for _ in range(2)]
        r_gs = [Res("gs0"), Res("gs1")]
        pen = [self.sb([128, 16, 16], BF16, "pen") for _ in range(2)]
        r_pen = [Res("pen0"), Res("pen1")]
        mx = self.sb([128, 16, 8], F32, "mx")
        r_mx = Res("mx")
        msk = self.sb([128, 16, 16], F32, "msk")
        r_msk = Res("msk")
        pst = [self.sb([16, 16, 512], BF16, "pst") for _ in range(2)]
        r_pst = [Res("pst0"), Res("pst1")]
        for b in range(2):
            self.op("pool", lambda e, b=b: e.memset(gs[b][:], -1e30), [], [r_gs[b]])
        ptv = self.PT.rearrange("(c r) n s -> n r c s", r=2)
        for qi in range(NQ):
            qblk = qi // 2
            ne = qblk
            b = qi % 2
            gb0, gb1 = 2 * b, 2 * b + 1
            g, rg = gs[b], r_gs[b]
            pn, rpn = pen[b], r_pen[b]
            ps0 = self.psum[gb0][:, 0:128]
            ps1 = self.psum[gb1][:, 0:128]

            def f_g(e, qi=qi, ps0=ps0, ps1=ps1):
                ins = None
                for par in range(2):
                    ps = ps0 if par == 0 else ps1
                    p0 = 64 * par
                    for c in range(8):
                        ins = e.matmul(ps[:, c * 16:(c + 1) * 16], lhsT=QTa[p0:p0 + 64, c, qi * 128:(qi + 1) * 128],
                                       rhs=KMb[p0:p0 + 64, c, :], start=True, stop=True, skip_group_check=True)
                return ins
            self.op("pool", lambda e, pn=pn: e.memset(pn[:], NEG_BIG), [], [rpn])
            self.op("pool", lambda e, pn=pn, qblk=qblk: e.memset(pn[:, :, qblk:qblk + 1], 0.0), [], [rpn])
            if 1 <= ne <= 3:
                self.op("pool", lambda e, pn=pn, ne=ne: e.memset(pn[:, :, 0:ne], 0.0), [], [rpn])
            elif ne > 3:
                self.op("pe", f_g, [r_qta, r_kmb], [self.pres[gb0], self.pres[gb1]])
                for par in range(2):
                    psv = (ps0 if par == 0 else ps1).rearrange("p (h n) -> p h n", n=16)
                    self.op("dve", lambda e, g=g, psv=psv, ne=ne, par=par: e.tensor_copy(
                        out=g[:, 8 * par:8 * par + 8, 0:ne], in_=psv[:, :, 0:ne]),
                        [self.pres[gb0 + par]], [rg])
                for h in range(16):
                    self.op("dve", lambda e, g=g, h=h: e.max(out=mx[:, h, :], in_=g[:, h, :]), [rg], [r_mx])
                self.op("dve", lambda e, g=g, ne=ne: e.tensor_tensor(
                    out=msk[:, :, 0:ne], in0=g[:, :, 0:ne], in1=mx[:, :, 2:3].to_broadcast([128, 16, ne]),
                    op=ALU.is_lt), [rg, r_mx], [r_msk])
                self.op("dve", lambda e, pn=pn, ne=ne: e.tensor_scalar(
                    out=pn[:, :, 0:ne], in0=msk[:, :, 0:ne], scalar1=NEG_BIG, scalar2=None, op0=ALU.mult),
                    [r_msk, rpn], [rpn])
            tb, tb2 = 4 + b, 6 + b
            tpsA = self.psum[tb][0:16, :].bitcast(BF16).rearrange("p (h q) -> p h q", q=128)
            tpsB = self.psum[tb2][0:16, :].bitcast(BF16).rearrange("p (h q) -> p h q", q=128)

            def f_t(e, pn=pn, tpsA=tpsA, tpsB=tpsB):
                ins = None
                for h in range(16):
                    o = tpsA[:, h, :] if h < 8 else tpsB[:, h - 8, :]
                    ins = e.transpose(out=o, in_=pn[:, h, :], identity=self.identb[:])
                return ins
            self.op("pe", f_t, [rpn, self.r_const], [self.pres[tb], self.pres[tb2]])
            qt = qi // 4
            st, rst = pst[qt % 2], r_pst[qt % 2]
            sub = qi % 4
            self.op("act", lambda e, st=st, tpsA=tpsA, sub=sub: e.copy(out=st[:, 0:8, sub * 128:(sub + 1) * 128], in_=tpsA),
                    [self.pres[tb]], [rst])
            self.op("act", lambda e, st=st, tpsB=tpsB, sub=sub: e.copy(out=st[:, 8:16, sub * 128:(sub + 1) * 128], in_=tpsB),
                    [self.pres[tb2]], [rst])
            if sub == 3:
                for r in range(2):
                    self.dma("sp", ptv[:, r, :, qt * 512:(qt + 1) * 512], st[:, 8 * r:8 * r + 8, :], [rst], [self.r_pt])
        self.s.barrier()
        self.release(m0)

    def oproj_phase(self, wo):
        S = self.S
        T = 512
        NT = S // T
        m0 = self.mark()
        Wo, r_wo = wo
        OTs = [self.sb([128, NKC, T], BF16, "OTs") for _ in range(2)]
        r_ots = [Res("ots0"), Res("ots1")]
        hres = [self.sb([128, D], F32, "hres") for _ in range(2)]
        r_hres = [Res("hres0"), Res("hres1")]
        otv = self.OT.rearrange("(k p) s -> p k s", p=128)
        yc = 0
        hc = 0

        def load(t):
            self.dma("sp", OTs[t % 2][:], otv[:, :, t * T:(t + 1) * T], [self.r_ot], [r_ots[t % 2]])
        load(0)
        for t in range(NT):
            if t + 1 < NT:
                load(t + 1)
            ot, rot = OTs[t % 2], r_ots[t % 2]
            for ts in range(4):
                hb, rhb = hres[hc % 2], r_hres[hc % 2]
                hc += 1
                r0 = t * T + ts * 128
                self.dma("sp", hb[:], self.h[r0:r0 + 128, :], [self.r_h], [rhb])
                for half in range(2):
                    bk = yc % 4
                    yc += 1
                    yps = self.psum[bk][:, :]

                    def f_mm(e, ts=ts, half=half, yps=yps, ot=ot):
                        ins = None
                        for kc in range(NKC):
                            ins = e.matmul(yps, lhsT=ot[:, kc, ts * 128:(ts + 1) * 128],
                                           rhs=Wo[:, kc, half * 512:(half + 1) * 512],
                                           start=(kc == 0), stop=(kc == NKC - 1))
                        return ins
                    self.op("pe", f_mm, [rot, r_wo], [self.pres[bk]])
                    self.op("dve", lambda e, hb=hb, half=half, yps=yps: e.tensor_tensor(
                        out=hb[:, half * 512:(half + 1) * 512], in0=yps, in1=hb[:, half * 512:(half + 1) * 512],
                        op=ALU.add), [self.pres[bk], rhb], [rhb])
                self.dma("sp", self.h[r0:r0 + 128, :], hb[:], [rhb], [self.r_h])
        self.s.barrier()
        self.release(m0)

    def final_norm_phase(self):
        S = self.S
        m0 = self.mark()
        gb = self.sb([128, D], F32, "gb")
        r_gb = Res("gb")
        self.load_gain(gb, r_gb, self.final_g[0:1, :])
        hb = [self.sb([128, D], F32, "fh") for _ in range(2)]
        jk = self.sb([128, D], BF16, "fj")
        fs = [self.sb([128, 1], F32, "fs") for _ in range(2)]
        r_hb = [Res("fh0"), Res("fh1")]
        for t in range(S // 128):
            b, rb = hb[t % 2], r_hb[t % 2]
            f = fs[t % 2]
            self.dma("sp", b[:], self.h[t * 128:(t + 1) * 128, :], [self.r_h], [rb])
            self.op("act", lambda e, b=b, f=f: e.activation(out=jk[:], in_=b[:], func=AF.Square, accum_out=f[:]), [rb], [rb])

            self.op("pool", lambda e, f=f: e.tensor_scalar(out=f[:], in0=f[:], scalar1=1.0 / D, scalar2=RMS_EPS, op0=ALU.mult,
                                                           op1=ALU.add), [rb], [rb])
            self.op("pool", lambda e, f=f: e.tensor_tensor(out=f[:], in0=f[:], in1=self.neghalf[:], op=ALU.pow),
                    [rb, self.r_const], [rb])
            self.op("dve", lambda e, b=b, f=f: e.scalar_tensor_tensor(out=b[:], in0=b[:], scalar=f[:], in1=gb[:],
                                                                      op0=ALU.mult, op1=ALU.mult), [rb, r_gb], [rb])
            self.dma("sp", self.out[t * 128:(t + 1) * 128, :], b[:], [rb], [self.r_h])
        self.s.barrier()
        self.release(m0)

    def build(self):
        self.declare_io()
        self.setup_consts()
        sa = self.stop_after
        self.ffn_phase(self.x, 0, 0, final=(sa == "ffn0"))
        if sa == "ffn0":
            return self.finish()
        self.setup_attn_consts()
        self.qkv_phase(0, "diff")
        if sa == "qkv0":
            return self.finish()
        base = self.mark()
        wo = self.wo_load(0)
        self.attn_phase("diff")
        if sa == "attn0":
            return self.finish()
        if sa == "att0":
            self.oproj_phase(wo)
            self.final_norm_phase()
            return self.finish()
        self.ffn_phase(self.h, 0, 1, oproj=wo)
        self.release(base)
        self.ffn_phase(self.h, 1, 0)
        self.qkv_phase(1, "moba")
        self.gate_phase()
        if sa == "gate1":
            return self.finish()
        base = self.mark()
        wo = self.wo_load(1)
        self.attn_phase("moba")
        if sa == "attn1":
            return self.finish()
        if sa == "att1":
            self.oproj_phase(wo)
            self.final_norm_phase()
            return self.finish()
        self.ffn_phase(self.h, 1, 1, final=True, oproj=wo)
        self.release(base)
        return self.finish()

    def finish(self):
        self.s.barrier()
        self.s.emit(self.nc)
        return self.nc


def host_consts(S):
    ident = np.eye(128, dtype=np.float32)
    anti = np.ascontiguousarray(ident[::-1])
    d = np.arange(256)
    n = np.maximum(d, 0)
    nf = np.maximum(n, 1).astype(np.float32)
    large = 16 + (np.log(nf / np.float32(16)) / np.float32(np.log(128 / 16)) * np.float32(16)).astype(np.int32)
    large = np.minimum(large, 31)
    bucket = np.where(n < 16, n, large)
    oh = np.zeros((32, 256), np.float32)
    oh[bucket, d] = 1.0
    ind = np.zeros((16, S), np.float32)
    for b in range(S // 256):
        ind[b, b * 256:(b + 1) * 256] = 1.0
    return {"c_ident": ident, "c_anti": anti, "c_oh": oh, "c_ind": ind}


_CACHE = {}


def run(inputs, S, ncores, stop_after=None, trace=False, debug=False):
    key = (S, stop_after, debug)
    if key not in _CACHE:
        _CACHE[key] = Builder(S, stop_after, debug).build()
    nc = _CACHE[key]
    consts = host_consts(S)
    f = lambda a: np.ascontiguousarray(a, dtype=np.float32)
    shared = {
        "rel_bias": f(inputs["rel_bias"]),
        "norm_g": f(inputs["norm_g"]),
        "final_norm_g": f(inputs["final_norm_g"]).reshape(1, D),
        "ffn_w_in": f(inputs["ffn_w_in"]),
        "ffn_w_out": f(inputs["ffn_w_out"]),
        "diff_w_qkv": f(inputs["diff_w_qkv"][0]),
        "diff_lambda": f(inputs["diff_lambda"][0]).reshape(1, 256),
        "diff_subln_g": f(inputs["diff_subln_g"][0]).reshape(1, 128),
        "diff_w_o": f(inputs["diff_w_o"][0]),
        "moba_w_qkv": f(inputs["moba_w_qkv"][0]),
        "moba_w_o": f(inputs["moba_w_o"][0]),
    }
    shared.update(consts)
    x = f(inputs["x"])
    in_maps = []
    for c in range(ncores):
        m = dict(shared)
        m["x"] = x[c]
        in_maps.append(m)
    res = run_bass_kernel_spmd(nc, in_maps, core_ids=list(range(ncores)), trace=trace)
    out = np.stack([np.asarray(r["out"]) for r in res.results], axis=0)
    return out, res


def kernel(**inputs):
    out, _ = run(inputs, SEQ, NCORES)
    return out.astype(np.float32)
```
